# Optimizing a Trainium2 kernel written in Bass

```python
import jax, jax.numpy as jnp
from jax import lax
import numpy as np

D_MODEL = 2048
BATCH = 4
SEQ = 2048
DEPTH = 2
DEC_BATCH = 128
DEC_SEQ = 1
PAST_LEN = 16384
PAGE_SIZE = 128

N_MIXERS = 2
N_CONV_LAYERS = (DEPTH + 1) // 2
N_RWKV_LAYERS = DEPTH // 2
CONV_W = 3
HEAD_DIM = 64
N_HEADS = D_MODEL // HEAD_DIM
DECAY_LORA = max(32, int(round(1.8 * D_MODEL ** 0.5 / 32)) * 32)
AAA_LORA = max(32, int(round(1.8 * D_MODEL ** 0.5 / 32)) * 32)
GATE_LORA = max(32, int(round(0.6 * D_MODEL ** 0.8 / 32)) * 32)
D_FF = -(-8 * D_MODEL // (3 * 256)) * 256
RMS_EPS = 1e-6
GN_EPS = HEAD_DIM * 1e-5
N_MIX_VECS = 6

kernel_name = "hybrid_shortconv_rwkv7_step"


def rms_norm(x, g):
    xf = x.astype(jnp.float32)
    y = xf * lax.rsqrt(jnp.mean(xf * xf, axis=-1, keepdims=True) + RMS_EPS)
    return (y * g.astype(jnp.float32)).astype(x.dtype)


def swiglu(x, w_gate, w_up, w_down):
    h = jax.nn.silu(jnp.einsum("btd,df->btf", x, w_gate)) * jnp.einsum("btd,df->btf", x, w_up)
    return jnp.einsum("btf,fd->btd", h, w_down)


def short_conv_mixer(x, buf, w_in, w_conv, w_out):
    T = x.shape[1]
    bch = jnp.einsum("btd,de->bte", x, w_in)
    b_gate, c_gate, h = jnp.split(bch, 3, axis=-1)
    u = c_gate * h
    u_pad = jnp.concatenate([buf.astype(u.dtype), u], axis=1)
    conv = u_pad[:, 0:T] * w_conv[0]
    for j in range(1, CONV_W):
        conv = conv + u_pad[:, j:j + T] * w_conv[j]
    y = jnp.einsum("btd,de->bte", b_gate * conv, w_out)
    return y, u_pad[:, T:]


def wkv7_scan(r, decay, k, v, a_vec, b_vec, s0):
    def step(s, inp):
        r_t, w_t, k_t, v_t, a_t, b_t = inp
        sa = jnp.einsum("bhvk,bhk->bhv", s, a_t)
        s = s * w_t[:, :, None, :] + sa[..., None] * b_t[:, :, None, :] + v_t[..., None] * k_t[:, :, None, :]
        y_t = jnp.einsum("bhvk,bhk->bhv", s, r_t)
        return s, y_t
    xs = tuple(jnp.moveaxis(t, 1, 0) for t in (r, decay, k, v, a_vec, b_vec))
    s_final, ys = lax.scan(step, s0, xs)
    return jnp.moveaxis(ys, 0, 1), s_final


def rwkv7_mixer(x, shift_prev, s0, mix, w_r, w_k, w_v, w_o, w0, w1, w2, a0, a1, a2,
                g1, g2, k_k, k_a, r_k, ln_w, ln_b):
    B, T, D = x.shape
    x_prev = jnp.concatenate([shift_prev[:, None].astype(x.dtype), x[:, :-1]], axis=1)
    xx = x_prev - x
    xr, xw, xk, xv, xa, xg = [x + xx * mix[j] for j in range(N_MIX_VECS)]
    r = xr @ w_r
    k = xk @ w_k
    v = xv @ w_v
    w_log = -jax.nn.softplus(-(w0 + jnp.tanh(xw @ w1) @ w2)) - 0.5
    decay = jnp.exp(-jnp.exp(w_log.astype(jnp.float32)))
    a = jax.nn.sigmoid(a0 + (xa @ a1) @ a2)
    g = jax.nn.sigmoid(xg @ g1) @ g2

    def heads(t):
        return t.reshape(B, T, N_HEADS, HEAD_DIM).astype(jnp.float32)

    kk = heads(k * k_k)
    kk = kk * lax.rsqrt(jnp.maximum(jnp.sum(kk * kk, axis=-1, keepdims=True), 1e-24))
    k = k * (1 + (a - 1) * k_a)
    rh, kh, vh, ah = heads(r), heads(k), heads(v), heads(a)
    y, s_final = wkv7_scan(rh, heads(decay), kh, vh, -kk, kk * ah, s0.astype(jnp.float32))
    mu = jnp.mean(y, axis=-1, keepdims=True)
    var = jnp.mean(jnp.square(y - mu), axis=-1, keepdims=True)
    y = ((y - mu) * lax.rsqrt(var + GN_EPS)).reshape(B, T, D) * ln_w.astype(jnp.float32) + ln_b.astype(jnp.float32)
    bonus = jnp.sum(rh * kh * r_k.astype(jnp.float32), axis=-1, keepdims=True) * vh
    y = (y + bonus.reshape(B, T, D)).astype(x.dtype)
    out = (y * g) @ w_o
    return out, x[:, -1], s_final


def setup_inputs(seed: int = 0) -> dict:
    key = jax.random.key(seed)
    ks = iter(jax.random.split(key, 48))
    D, H, N, NC, NR = D_MODEL, N_HEADS, HEAD_DIM, N_CONV_LAYERS, N_RWKV_LAYERS

    def nrm(shape, scale):
        return scale * jax.random.normal(next(ks), shape, jnp.float32)

    return {
        "x_prompt": nrm((BATCH, SEQ, D), 1.0),
        "x_sample": nrm((DEC_BATCH, DEC_SEQ, D), 1.0),
        "cache_conv": nrm((NC, DEC_BATCH, CONV_W - 1, D), 0.5),
        "state_wkv": nrm((NR, DEC_BATCH, H, N, N), 0.1),
        "state_shift": nrm((NR, DEC_BATCH, D), 1.0),
        "norm_mix_pre": 1.0 + nrm((DEPTH, D), 0.05),
        "norm_mix_post": 1.0 + nrm((DEPTH, D), 0.05),
        "norm_ffn_pre": 1.0 + nrm((DEPTH, D), 0.05),
        "norm_ffn_post": 1.0 + nrm((DEPTH, D), 0.05),
        "conv_w_in": nrm((NC, D, 3 * D), D ** -0.5),
        "conv_w": nrm((NC, CONV_W, D), CONV_W ** -0.5),
        "conv_w_out": nrm((NC, D, D), D ** -0.5),
        "rwkv_mix": jax.random.uniform(next(ks), (NR, N_MIX_VECS, D), jnp.float32),
        "rwkv_w_r": nrm((NR, D, D), D ** -0.5),
        "rwkv_w_k": nrm((NR, D, D), D ** -0.5),
        "rwkv_w_v": nrm((NR, D, D), D ** -0.5),
        "rwkv_w_o": nrm((NR, D, D), D ** -0.5),
        "rwkv_w0": nrm((NR, D), 0.5) - 1.0,
        "rwkv_w1": nrm((NR, D, DECAY_LORA), D ** -0.5),
        "rwkv_w2": nrm((NR, DECAY_LORA, D), 0.1 * DECAY_LORA ** -0.5),
        "rwkv_a0": nrm((NR, D), 0.1),
        "rwkv_a1": nrm((NR, D, AAA_LORA), D ** -0.5),
        "rwkv_a2": nrm((NR, AAA_LORA, D), 0.1 * AAA_LORA ** -0.5),
        "rwkv_g1": nrm((NR, D, GATE_LORA), D ** -0.5),
        "rwkv_g2": nrm((NR, GATE_LORA, D), GATE_LORA ** -0.5),
        "rwkv_k_k": 0.85 + nrm((NR, D), 0.1),
        "rwkv_k_a": 1.0 + nrm((NR, D), 0.1),
        "rwkv_r_k": nrm((NR, H, N), 0.1),
        "rwkv_ln_w": 1.0 + nrm((NR, D), 0.05),
        "rwkv_ln_b": nrm((NR, D), 0.01),
        "ffn_w_gate": nrm((DEPTH, D, D_FF), D ** -0.5),
        "ffn_w_up": nrm((DEPTH, D, D_FF), D ** -0.5),
        "ffn_w_down": nrm((DEPTH, D_FF, D), D_FF ** -0.5),
    }


def reference(x_prompt, x_sample, cache_conv, state_wkv, state_shift,
              norm_mix_pre, norm_mix_post, norm_ffn_pre, norm_ffn_post,
              conv_w_in, conv_w, conv_w_out,
              rwkv_mix, rwkv_w_r, rwkv_w_k, rwkv_w_v, rwkv_w_o,
              rwkv_w0, rwkv_w1, rwkv_w2, rwkv_a0, rwkv_a1, rwkv_a2,
              rwkv_g1, rwkv_g2, rwkv_k_k, rwkv_k_a, rwkv_r_k, rwkv_ln_w, rwkv_ln_b,
              ffn_w_gate, ffn_w_up, ffn_w_down):

    def run(x, conv_bufs, wkv_states, shift_states):
        new_conv, new_wkv, new_shift = [], [], []
        for i in range(DEPTH):
            j = i // N_MIXERS
            h = rms_norm(x, norm_mix_pre[i])
            if i % N_MIXERS == 0:
                m, buf = short_conv_mixer(h, conv_bufs[j], conv_w_in[j], conv_w[j], conv_w_out[j])
                new_conv.append(buf)
            else:
                m, sh, st = rwkv7_mixer(h, shift_states[j], wkv_states[j], rwkv_mix[j],
                                        rwkv_w_r[j], rwkv_w_k[j], rwkv_w_v[j], rwkv_w_o[j],
                                        rwkv_w0[j], rwkv_w1[j], rwkv_w2[j],
                                        rwkv_a0[j], rwkv_a1[j], rwkv_a2[j],
                                        rwkv_g1[j], rwkv_g2[j], rwkv_k_k[j], rwkv_k_a[j],
                                        rwkv_r_k[j], rwkv_ln_w[j], rwkv_ln_b[j])
                new_shift.append(sh)
                new_wkv.append(st)
            x = x + rms_norm(m, norm_mix_post[i])
            f = swiglu(rms_norm(x, norm_ffn_pre[i]), ffn_w_gate[i], ffn_w_up[i], ffn_w_down[i])
            x = x + rms_norm(f, norm_ffn_post[i])
        return x, jnp.stack(new_conv), jnp.stack(new_wkv), jnp.stack(new_shift)

    b_p = x_prompt.shape[0]
    conv0 = jnp.zeros((N_CONV_LAYERS, b_p, CONV_W - 1, D_MODEL), x_prompt.dtype)
    wkv0 = jnp.zeros((N_RWKV_LAYERS, b_p, N_HEADS, HEAD_DIM, HEAD_DIM), jnp.float32)
    shift0 = jnp.zeros((N_RWKV_LAYERS, b_p, D_MODEL), x_prompt.dtype)
    y_prompt, conv_p, wkv_p, shift_p = run(x_prompt, conv0, wkv0, shift0)
    y_sample, conv_s, wkv_s, shift_s = run(x_sample, cache_conv, state_wkv, state_shift)
    return (y_prompt, y_sample, conv_p, conv_s, wkv_p, wkv_s, shift_p, shift_s)
```

```python
import numpy as np
from contextlib import ExitStack
import concourse.bass as bass
import concourse.mybir as mybir
from concourse.bass_utils import run_bass_kernel_spmd

F32 = mybir.dt.float32
BF16 = mybir.dt.bfloat16
ALU = mybir.AluOpType
AF = mybir.ActivationFunctionType
AX = mybir.AxisListType

D = 2048
KC = 16
DFF = 5632
FC = 44
TT = 512
NTILE = 4
NS = 16
NCOL = TT + NS
CH = 64
NV = 32
EPS = 1e-6
GN_EPS = 64 * 1e-5

V_MIXPRE, V_MIXPOST, V_FFNPRE, V_FFNPOST = 0, 2, 4, 6
V_CONVW = 8
V_MIX = 11
V_W0, V_A0, V_KK, V_KA, V_RK, V_LNW, V_LNB = 17, 18, 19, 20, 21, 22, 23
V_OMM = 24


class TB:
    def __init__(self, t, name):
        self.t = t
        self.name = name
        self.w = None
        self.r = {}
        self.dsem = None
        self.ndma = 0


class Sched:
    ENG = ["pe", "act", "dve", "pool", "sp"]

    def __init__(self, nc, es):
        self.nc = nc
        self.es = es
        self.prog = {e: [] for e in self.ENG}
        self.cnt = {e: 0 for e in self.ENG}
        self.sem = {e: es.enter_context(nc.semaphore("sem_" + e)) for e in self.ENG if e != "sp"}
        self.waited = {e: {} for e in self.ENG}
        self.banks = []
        self.bank_i = 0
        self.nsem = 0

    def sbuf(self, name, shape, dt=F32):
        t = self.es.enter_context(self.nc.sbuf_tensor("sb_" + name, list(shape), dt))
        return TB(t, name)

    def mkbanks(self):
        for i in range(8):
            t = self.es.enter_context(self.nc.psum_tensor("bank%d" % i, [128, 512], F32))
            self.banks.append(TB(t, "bank%d" % i))

    def bank(self):
        b = self.banks[self.bank_i % 5]
        self.bank_i += 1
        return b

    @staticmethod
    def _exp(bs):
        out = []
        for b in bs:
            out.append(b)
            out.extend(getattr(b, "aliases", ()))
        return out

    def _collect(self, eng, reads, writes, is_dma):
        evs = []
        for b in reads:
            if b.w is not None:
                evs.append(b.w + ("raw",))
        for b in writes:
            if b.w is not None:
                evs.append(b.w + ("waw",))
            for ev in b.r.values():
                evs.append(ev + ("war",))
        waits = {}
        for (key, sem, val, src, kind) in evs:
            if src == eng and not is_dma and key == "e_" + eng:
                if eng == "pe":
                    continue
                if kind != "raw":
                    continue
            if self.waited[eng].get(key, 0) >= val:
                continue
            if key not in waits or waits[key][1] < val:
                waits[key] = (sem, val)
        for key, (sem, val) in waits.items():
            self.waited[eng][key] = val
        return list(waits.values())

    def op(self, eng, fn, reads=(), writes=()):
        reads, writes = self._exp(reads), self._exp(writes)
        waits = self._collect(eng, reads, writes, False)
        self.cnt[eng] += 1
        ev = ("e_" + eng, self.sem[eng], self.cnt[eng], eng)
        self.prog[eng].append((waits, fn, (self.sem[eng], 1)))
        for b in reads:
            b.r[ev[0]] = ev
        for b in writes:
            b.w = ev
            b.r = {}

    def dma(self, out, in_, reads=(), writes=(), sembuf=None, eng="sp"):
        sb = sembuf
        reads, writes = self._exp(reads), self._exp(writes)
        if sb.dsem is None:
            sb.dsem = self.es.enter_context(self.nc.semaphore("dsem%d" % self.nsem))
            sb.dkey = "d%d" % self.nsem
            self.nsem += 1
        waits = self._collect(eng, reads, writes, True)
        if sb.ndma > 0 and self.waited[eng].get(sb.dkey, 0) < 16 * sb.ndma:
            waits.append((sb.dsem, 16 * sb.ndma))
            self.waited[eng][sb.dkey] = 16 * sb.ndma
        sb.ndma += 1
        ev = (sb.dkey, sb.dsem, 16 * sb.ndma, "dma")
        self.prog[eng].append((waits, lambda e: e.dma_start(out=out, in_=in_), (sb.dsem, 16)))
        for b in reads:
            b.r[ev[0]] = ev
        for b in writes:
            b.w = ev
            b.r = {}
        return ev

    def emit(self, block, final_waits):
        nc = self.nc

        def run(engname):
            def body(e):
                for (waits, fn, inc) in self.prog[engname]:
                    for (sem, val) in waits:
                        e.wait_ge(sem, val)
                    ins = fn(e)
                    ins.then_inc(inc[0], inc[1])
                if engname == "sp":
                    for (sem, val) in final_waits:
                        e.wait_ge(sem, val)
            return body

        block.tensor(run("pe"))
        block.scalar(run("act"))
        block.vector(run("dve"))
        block.gpsimd(run("pool"))
        block.sync(run("sp"))


NCST = 640 + 384


def build_program(n_tiles=NTILE):
    nc = bass.Bass("TRN2", target_bir_lowering=False)

    def din(name, shape):
        return nc.dram_tensor(name, list(shape), F32, kind="ExternalInput").ap()

    def dout(name, shape):
        return nc.dram_tensor(name, list(shape), F32, kind="ExternalOutput").ap()

    TSEQ = n_tiles * TT
    xp = din("xp", [TSEQ, D])
    xs = din("xs", [NS, D])
    cconv = din("cconv", [NS * 2, D])
    swkv = din("swkv", [NS, 32, 64, 64])
    sshift = din("sshift", [NS, D])
    vecs_d = din("vecs", [128, NV * KC])
    cst_d = din("cst", [128, NCST])
    w_in = din("w_in", [D, 3 * D])
    w_out = din("w_out", [D, D])
    w_r = din("w_r", [D, D])
    w_k = din("w_k", [D, D])
    w_v = din("w_v", [D, D])
    w_o = din("w_o", [D, D])
    w1 = din("w1", [D, 96])
    w2 = din("w2", [96, D])
    a1 = din("a1", [D, 96])
    a2 = din("a2", [96, D])
    g1 = din("g1", [D, 256])
    g2 = din("g2", [256, D])
    wg = [din("wg%d" % i, [D, DFF]) for i in range(2)]
    wu = [din("wu%d" % i, [D, DFF]) for i in range(2)]
    wd = [din("wd%d" % i, [DFF, D]) for i in range(2)]

    yp = dout("yp", [TSEQ, D])
    ysd = dout("ys", [NS, D])
    ncp = dout("ncp", [2, D])
    ncs = dout("ncs", [NS * 2, D])
    nwp = dout("nwp", [32, 64, 64])
    nws = dout("nws", [NS, 32, 64, 64])
    nsp = dout("nsp", [1, D])
    nss = dout("nss", [NS, D])

    with ExitStack() as es:
        S = Sched(nc, es)
        S.mkbanks()
        x = S.sbuf("x", [128, KC, NCOL])
        bufA = S.sbuf("bufA", [128, KC, NCOL])
        hb = S.sbuf("hb", [128, KC, NCOL], BF16)
        big = S.sbuf("big", [128, 3 * KC * NCOL], BF16)
        NWST, NWBF = 6, 2
        wst = [S.sbuf("wst%d" % i, [128, 4 * 128]) for i in range(NWST)]
        wbf = [S.sbuf("wbf%d" % i, [128, 16 * 128], BF16) for i in range(NWBF)]
        xin = S.sbuf("xin", [128, D])
        vecs = S.sbuf("vecs", [128, NV, KC])
        cst = S.sbuf("cst", [128, NCST])
        onesb = S.sbuf("onesb", [128, 128], BF16)
        onesf = S.sbuf("onesf", [128, 64])
        epsb = S.sbuf("epsb", [128, 2])
        rstd = S.sbuf("rstd", [128, NCOL])
        tmpc = S.sbuf("tmpc", [128, NCOL + 2])
        tmpd = S.sbuf("tmpd", [128, NCOL + 2])
        ucarry = S.sbuf("ucarry", [128, KC, 2])
        cacheT = S.sbuf("cacheT", [128, KC, 2 * NS])
        newcs = S.sbuf("newcs", [128, KC, 2 * NS])
        shcarry = S.sbuf("shcarry", [128, KC, 1])
        shiftT = S.sbuf("shiftT", [128, KC, NS])
        S0T = S.sbuf("S0T", [128, KC, 64])
        lora = S.sbuf("lora", [128, 4, NCOL], BF16)
        wcT = S.sbuf("wcT", [128, TT // CH])
        EX = [S.sbuf("EX%d" % i, [128, 640], BF16) for i in range(3)]
        TK = [S.sbuf("TK%d" % i, [128, 320], BF16) for i in range(3)]
        MS = [S.sbuf("MS%d" % i, [128, 640], BF16) for i in range(3)]
        PMA = [S.sbuf("PMA%d" % i, [128, 256], BF16) for i in range(3)]
        PMB = [S.sbuf("PMB%d" % i, [128, 256], BF16) for i in range(3)]
        XW = [S.sbuf("XW%d" % i, [128, 192]) for i in range(3)]
        XWB = [S.sbuf("XWB%d" % i, [128, 192], BF16) for i in range(3)]
        TAT = [S.sbuf("TAT%d" % i, [128, 128], BF16) for i in range(3)]
        UU = [S.sbuf("UU%d" % i, [128, 64], BF16) for i in range(2)]
        AB3 = S.sbuf("AB3", [128, 3 * TT], BF16)
        S0Tb = S.sbuf("S0Tb", [128, 64], BF16)
        identb = S.sbuf("identb", [128, 192], BF16)
        YX = [S.sbuf("YX%d" % i, [128, 128]) for i in range(2)]

        ident = cst.t[:, 0:128]
        BDm = cst.t[:, 128:256]
        m_su = cst.t[0:64, 256:320]
        m_si = cst.t[0:64, 320:384]
        m_sl = cst.t[0:64, 384:448]
        shid = cst.t[0:64, 448:576]
        ident2 = cst.t[:, 576:640]
        msu_bd = cst.t[:, 640:768]
        msi_bd = cst.t[:, 768:896]
        msl_bd = cst.t[:, 896:1024]

        def V(i, c):
            return vecs.t[:, i, c:c + 1]

        S.dma(vecs.t[:].rearrange("p v c -> p (v c)"), vecs_d, writes=[vecs], sembuf=vecs)
        S.dma(cst.t[:], cst_d, writes=[cst], sembuf=cst)
        S.op("dve", lambda e: e.memset(onesb.t[:], 1.0), writes=[onesb])
        S.op("dve", lambda e: e.tensor_copy(out=identb.t[:, 0:128], in_=cst.t[:, 0:128]), reads=[cst], writes=[identb])
        S.op("dve", lambda e: e.tensor_copy(out=identb.t[:, 128:192], in_=cst.t[:, 576:640]), reads=[cst], writes=[identb])
        S.op("dve", lambda e: e.memset(onesf.t[:], 1.0), writes=[onesf])
        S.op("dve", lambda e: e.memset(epsb.t[:, 0:1], EPS), writes=[epsb])
        S.op("dve", lambda e: e.memset(epsb.t[:, 1:2], GN_EPS), writes=[epsb])
        S.op("dve", lambda e: e.memset(ucarry.t[:], 0.0), writes=[ucarry])
        S.op("dve", lambda e: e.memset(shcarry.t[:], 0.0), writes=[shcarry])
        S.op("dve", lambda e: e.memset(S0T.t[:], 0.0), writes=[S0T])
        S.op("dve", lambda e: e.tensor_scalar(out=vecs.t[:, V_OMM:V_OMM + 6, :], in0=vecs.t[:, V_MIX:V_MIX + 6, :],
                                              scalar1=-1.0, scalar2=1.0, op0=ALU.mult, op1=ALU.add),
             reads=[vecs], writes=[vecs])

        def segs_of(tile):
            return [(0, TT)] + ([(TT, NCOL)] if tile == 0 else [])

        def ncols_of(tile):
            return NCOL if tile == 0 else TT

        wctr = [0, 0]

        def load_w_dma(src3, kp, kcn, n, engs=("pool", "act")):
            parts = []
            k0 = 0
            while k0 < kcn:
                kk_ = min(4, kcn - k0)
                st = wst[wctr[1] % NWST]
                eng = engs[wctr[1] % 2]
                wctr[1] += 1
                sv = st.t[0:kp, 0:kk_ * n].rearrange("p (k n) -> p k n", n=n)
                S.dma(sv, src3[:, k0:k0 + kk_, :], writes=[st], sembuf=st)
                parts.append((st, sv, k0, kk_, eng))
                k0 += kk_
            return (parts, kp, kcn, n)

        def load_w_cast(h):
            parts, kp, kcn, n = h
            wb = wbf[wctr[0] % NWBF]
            wctr[0] += 1
            dstb = wb.t[0:kp, 0:kcn * n].rearrange("p (k n) -> p k n", n=n)
            for (st, sv, k0, kk_, eng) in parts:
                if eng in ("pool", "dve"):
                    S.op(eng, lambda e, a=dstb[:, k0:k0 + kk_, :], b=sv: e.tensor_copy(out=a, in_=b),
                         reads=[st], writes=[wb])
                else:
                    S.op("act", lambda e, a=dstb[:, k0:k0 + kk_, :], b=sv: e.activation(out=a, in_=b, func=AF.Copy),
                         reads=[st], writes=[wb])
            return dstb, wb

        def load_w(src3, kp, kcn, n):
            return load_w_cast(load_w_dma(src3, kp, kcn, n))

        DENSE_CAST = ("dve", "act")

        def wstream(specs):
            n = len(specs)
            st = {"i": 0, "h": {}, "w": {}}
            st["h"][0] = load_w_dma(*specs[0], engs=DENSE_CAST)
            st["w"][0] = load_w_cast(st["h"].pop(0))
            if n > 1:
                st["h"][1] = load_w_dma(*specs[1], engs=DENSE_CAST)

            def get():
                i = st["i"]
                st["i"] += 1
                if i + 1 < n:
                    st["w"][i + 1] = load_w_cast(st["h"].pop(i + 1))
                if i + 2 < n:
                    st["h"][i + 2] = load_w_dma(*specs[i + 2], engs=DENSE_CAST)
                return st["w"].pop(i)
            return get

        def wsrc(w, k0c, kcn, c0, n):
            return w[k0c * 128:(k0c + kcn) * 128, c0:c0 + n].rearrange("(k p) n -> p k n", p=128)

        def mm_acc(bk, rows, wv, act_view, kp, kchunks, c0, c1, k_first=0, k_total=None, reads=()):
            k_total = kchunks if k_total is None else k_total

            def f(e):
                for k in range(kchunks):
                    ins = e.matmul(bk.t[0:rows, 0:c1 - c0], wv[0:kp, k, 0:rows], act_view[0:kp, k_first + k, c0:c1],
                                   start=(k_first + k == 0), stop=(k_first + k == k_total - 1))
                return ins
            S.op("pe", f, reads=list(reads), writes=[bk])

        def sumsq_rstd(src, tile):
            nco = ncols_of(tile)
            S.op("act", lambda e: e.activation(out=hb.t[:, :, 0:nco], in_=src.t[:, :, 0:nco], func=AF.Square),
                 reads=[src], writes=[hb])
            for (c0, c1) in segs_of(tile):
                bk = S.bank()

                def f(e, bk=bk, c0=c0, c1=c1):
                    for c in range(KC):
                        ins = e.matmul(bk.t[:, 0:c1 - c0], onesb.t[:], hb.t[:, c, c0:c1], start=(c == 0), stop=(c == KC - 1))
                    return ins
                S.op("pe", f, reads=[hb, onesb], writes=[bk])
                S.op("act", lambda e, bk=bk, c0=c0, c1=c1: e.activation(out=rstd.t[:, c0:c1], in_=bk.t[:, 0:c1 - c0], func=AF.Sqrt,
                                                                       bias=epsb.t[:, 0:1], scale=1.0 / D),
                     reads=[bk, epsb], writes=[rstd])
            S.op("dve", lambda e: e.reciprocal(out=rstd.t[:, 0:nco], in_=rstd.t[:, 0:nco]), reads=[rstd], writes=[rstd])

        def norm_to(dst, src, gvec, tile):
            nco = ncols_of(tile)
            sumsq_rstd(src, tile)

            def f(e):
                for c in range(KC):
                    ins = e.scalar_tensor_tensor(out=dst.t[:, c, 0:nco], in0=src.t[:, c, 0:nco], scalar=V(gvec, c),
                                                 in1=rstd.t[:, 0:nco], op0=ALU.mult, op1=ALU.mult)
                return ins
            S.op("dve", f, reads=[src, rstd, vecs], writes=[dst])

        def resid_add(src, gvec, tile):
            nco = ncols_of(tile)
            sumsq_rstd(src, tile)

            def f(e):
                for c in range(KC):
                    e.scalar_tensor_tensor(out=src.t[:, c, 0:nco], in0=src.t[:, c, 0:nco], scalar=V(gvec, c),
                                           in1=rstd.t[:, 0:nco], op0=ALU.mult, op1=ALU.mult)
                return e.tensor_tensor(out=x.t[:, :, 0:nco], in0=x.t[:, :, 0:nco], in1=src.t[:, :, 0:nco], op=ALU.add)
            S.op("dve", f, reads=[src, rstd, vecs, x], writes=[src, x])

        def lin(wmat, kchunks, ncols_out, act_view, act_tb, consume, tile):
            segs_ = segs_of(tile)
            wget = wstream([(wsrc(wmat, k0, min(16, kchunks - k0), g0, 128), 128, min(16, kchunks - k0), 128)
                            for g0 in range(0, ncols_out, 128) for k0 in range(0, kchunks, 16)])
            for g0 in range(0, ncols_out, 128):
                banks = [S.bank() for _ in segs_]
                for k0 in range(0, kchunks, 16):
                    kn = min(16, kchunks - k0)
                    wv, wb = wget()
                    for si, (c0, c1) in enumerate(segs_):
                        mm_acc(banks[si], 128, wv, act_view, 128, kn, c0, c1, k_first=k0, k_total=kchunks, reads=[wb, act_tb])
                for si, (c0, c1) in enumerate(segs_):
                    consume(g0 // 128, banks[si], c0, c1)

        def evac_to(dst):
            def consume(ci, bk, c0, c1):
                S.op("act", lambda e: e.activation(out=dst.t[:, ci, c0:c1], in_=bk.t[:, 0:c1 - c0], func=AF.Copy),
                     reads=[bk], writes=[dst])
            return consume

        def load_T(dst3, src_rows, nrows, col0, wr):
            S.dma(xin.t[0:nrows, :], src_rows, writes=[xin], sembuf=xin)
            for g in range(4):
                bk = S.bank()

                def f(e, bk=bk, g=g):
                    for j in range(4):
                        c = 4 * g + j
                        ins = e.transpose(bk.t[:, j * 128:j * 128 + nrows], xin.t[0:nrows, c * 128:(c + 1) * 128],
                                          ident[0:nrows, 0:nrows])
                    return ins
                S.op("pe", f, reads=[xin, cst], writes=[bk])
                S.op("act", lambda e, bk=bk, g=g: e.activation(
                    out=dst3[:, 4 * g:4 * g + 4, col0:col0 + nrows],
                    in_=bk.t[:, :].rearrange("p (j n) -> p j n", n=128)[:, :, 0:nrows], func=AF.Copy),
                    reads=[bk], writes=[wr])

        def store_T(dst_rows, src3, nrows, rd):
            for g in range(4):
                bk = S.bank()

                def f(e, bk=bk, g=g):
                    for j in range(4):
                        c = 4 * g + j
                        ins = e.transpose(bk.t[0:nrows, j * 128:(j + 1) * 128], src3(c), ident)
                    return ins
                S.op("pe", f, reads=[rd, cst], writes=[bk])
                S.op("act", lambda e, bk=bk, g=g: e.activation(out=xin.t[0:nrows, g * 512:(g + 1) * 512],
                                                               in_=bk.t[0:nrows, :], func=AF.Copy),
                     reads=[bk], writes=[xin])
            S.dma(dst_rows, xin.t[0:nrows, :], reads=[xin], sembuf=xin)

        load_T(cacheT.t, cconv, 2 * NS, 0, cacheT)
        load_T(shiftT.t, sshift, NS, 0, shiftT)

        flat = bufA.t[:].rearrange("p c n -> p (c n)")
        tq = [flat[:, i * NCOL:(i + 1) * NCOL] for i in range(14)]
        (R_, K_, V_, LW_, A_, G_, KK_, BON_, CW_, E_, Y_, T1_, T2_, T3_) = tq
        slot_tbs = [TB(None, "slot_" + n) for n in ("R", "K", "V", "LW", "A", "G", "KK", "BON", "CW", "E", "Y", "T1", "T2", "T3")]
        (tR, tK, tV0, tLW, tA, tG0, tKK, tBON, tCW, tE, tY, tT1, tT2, tT3) = slot_tbs
        bufA.aliases = slot_tbs
        Vv, tVv = [V_, tmpc.t[:, 0:NCOL]], [tV0, tmpc]
        Gv, tGv = [G_, tmpd.t[:, 0:NCOL]], [tG0, tmpd]
        xflat = xin.t[:, :]
        SIN = xflat[:, 0:512].rearrange("p (n k) -> p n k", k=64)
        TMS = xflat[:, 512:1024].rearrange("p (n k) -> p n k", k=64)
        DX = xflat[:, 1024:1536].rearrange("p (n k) -> p n k", k=64)
        SA = xflat[:, 1536:1544]

        def do_tile(tile):
            nco = ncols_of(tile)
            segs = segs_of(tile)
            t0 = tile * TT
            for blk in range(4):
                load_T(x.t, xp[t0 + blk * 128:t0 + (blk + 1) * 128, :], 128, blk * 128, x)
            if tile == 0:
                load_T(x.t, xs, NS, TT, x)

            cget = wstream([(wsrc(w_in, 0, 16, off + j * 128, 128), 128, 16, 128) for j in range(KC) for off in (D, 2 * D, 0)])
            norm_to(hb, x, V_MIXPRE + 0, tile)
            zbv = big.t[:, 0:KC * NCOL].rearrange("p (c n) -> p c n", n=NCOL)
            def conv_chunk(j):
                hbk = {}
                for which, off in (("c", D), ("h", 2 * D), ("b", 0)):
                    wv, wb = cget()
                    for si, (c0, c1) in enumerate(segs):
                        bk = S.bank()
                        mm_acc(bk, 128, wv, hb.t, 128, KC, c0, c1, reads=[wb, hb])
                        hbk[(which, si)] = bk
                        if which == "c":
                            S.op("act", lambda e, bk=bk, c0=c0, c1=c1: e.activation(
                                out=bufA.t[:, j, c0:c1], in_=bk.t[:, 0:c1 - c0], func=AF.Copy), reads=[bk], writes=[bufA])
                        elif which == "h":
                            S.op("dve", lambda e, bk=bk, c0=c0, c1=c1: e.tensor_tensor(
                                out=bufA.t[:, j, c0:c1], in0=bufA.t[:, j, c0:c1], in1=bk.t[:, 0:c1 - c0], op=ALU.mult),
                                reads=[bk, bufA], writes=[bufA])
                        elif si == 0:
                            def g(e, bk=bk):
                                e.tensor_copy(out=tmpc.t[:, 0:2], in_=ucarry.t[:, j, :])
                                e.tensor_copy(out=tmpc.t[:, 2:2 + TT], in_=bufA.t[:, j, 0:TT])
                                e.tensor_copy(out=ucarry.t[:, j, :], in_=bufA.t[:, j, TT - 2:TT])
                                e.tensor_scalar(out=tmpd.t[:, 0:TT], in0=tmpc.t[:, 0:TT], scalar1=V(V_CONVW + 0, j), scalar2=0.0, op0=ALU.mult, op1=ALU.add)
                                e.scalar_tensor_tensor(out=tmpd.t[:, 0:TT], in0=tmpc.t[:, 1:1 + TT], scalar=V(V_CONVW + 1, j),
                                                       in1=tmpd.t[:, 0:TT], op0=ALU.mult, op1=ALU.add)
                                e.scalar_tensor_tensor(out=tmpd.t[:, 0:TT], in0=tmpc.t[:, 2:2 + TT], scalar=V(V_CONVW + 2, j),
                                                       in1=tmpd.t[:, 0:TT], op0=ALU.mult, op1=ALU.add)
                                return e.tensor_tensor(out=zbv[:, j, 0:TT], in0=tmpd.t[:, 0:TT], in1=bk.t[:, 0:TT], op=ALU.mult)
                            S.op("dve", g, reads=[bk, bufA, ucarry, vecs, tmpc, tmpd], writes=[tmpc, tmpd, ucarry, big])
                        else:
                            cv = cacheT.t[:, j, :].rearrange("p (n s) -> p s n", s=2)
                            nv = newcs.t[:, j, :].rearrange("p (n s) -> p s n", s=2)
                            S.op("dve", lambda e: e.tensor_scalar(out=tmpd.t[:, 0:NS], in0=cv[:, 0, :], scalar1=V(V_CONVW + 0, j), scalar2=0.0,
                                                                  op0=ALU.mult, op1=ALU.add), reads=[cacheT, vecs], writes=[tmpd])
                            S.op("dve", lambda e: e.scalar_tensor_tensor(out=tmpd.t[:, 0:NS], in0=cv[:, 1, :], scalar=V(V_CONVW + 1, j),
                                                                         in1=tmpd.t[:, 0:NS], op0=ALU.mult, op1=ALU.add),
                                 reads=[cacheT, vecs, tmpd], writes=[tmpd])
                            S.op("dve", lambda e: e.scalar_tensor_tensor(out=tmpd.t[:, 0:NS], in0=bufA.t[:, j, TT:NCOL], scalar=V(V_CONVW + 2, j),
                                                                         in1=tmpd.t[:, 0:NS], op0=ALU.mult, op1=ALU.add),
                                 reads=[bufA, vecs, tmpd], writes=[tmpd])
                            S.op("dve", lambda e: e.tensor_copy(out=nv[:, 0, :], in_=cv[:, 1, :]), reads=[cacheT], writes=[newcs])
                            S.op("dve", lambda e: e.tensor_copy(out=nv[:, 1, :], in_=bufA.t[:, j, TT:NCOL]), reads=[bufA], writes=[newcs])
                            S.op("dve", lambda e, bk=bk: e.tensor_tensor(out=zbv[:, j, TT:NCOL], in0=tmpd.t[:, 0:NS], in1=bk.t[:, 0:NS], op=ALU.mult),
                                 reads=[bk, tmpd], writes=[big])
            for j_ in range(KC):
                conv_chunk(j_)
            lin(w_out, KC, D, zbv, big, evac_to(bufA), tile)
            resid_add(bufA, V_MIXPOST + 0, tile)
            if tile == n_tiles - 1:
                store_T(ncp, lambda c: ucarry.t[:, c, :], 2, ucarry)
            if tile == 0:
                store_T(ncs, lambda c: newcs.t[:, c, :], 2 * NS, newcs)

            def ffn(layer):
                fget = wstream([(wsrc(wm, 0, 16, j * 128, 128), 128, 16, 128) for j in range(FC) for wm in (wg[layer], wu[layer])])
                norm_to(hb, x, V_FFNPRE + layer, tile)
                hT = big.t[:, 0:FC * NCOL].rearrange("p (c n) -> p c n", n=NCOL)
                def fchunk(j):
                    for which, wm in (("g", wg[layer]), ("u", wu[layer])):
                        wv, wb = fget()
                        for si, (c0, c1) in enumerate(segs):
                            bk = S.bank()
                            mm_acc(bk, 128, wv, hb.t, 128, KC, c0, c1, reads=[wb, hb])
                            if which == "g":
                                S.op("act", lambda e, bk=bk, c0=c0, c1=c1: e.activation(
                                    out=tmpc.t[:, c0:c1], in_=bk.t[:, 0:c1 - c0], func=AF.Silu), reads=[bk], writes=[tmpc])
                            else:
                                S.op("dve", lambda e, bk=bk, c0=c0, c1=c1: e.tensor_tensor(
                                    out=hT[:, j, c0:c1], in0=tmpc.t[:, c0:c1], in1=bk.t[:, 0:c1 - c0], op=ALU.mult),
                                    reads=[bk, tmpc], writes=[big])
                for j_ in range(FC):
                    fchunk(j_)
                lin(wd[layer], FC, D, hT, big, evac_to(bufA), tile)
                resid_add(bufA, V_FFNPOST + layer, tile)

            ffn(0)

            norm_to(bufA, x, V_MIXPRE + 1, tile)
            if tile == n_tiles - 1:
                store_T(nsp, lambda c: bufA.t[:, c, TT - 1:TT], 1, bufA)
            if tile == 0:
                store_T(nss, lambda c: bufA.t[:, c, TT:NCOL], NS, bufA)
            mixv = [big.t[:, i * KC * NCOL:(i + 1) * KC * NCOL].rearrange("p (c n) -> p c n", n=NCOL) for i in range(3)]

            def make_mix(dst_view, dst_tb, mi):
                def f(e):
                    for c in range(KC):
                        e.tensor_scalar(out=tmpc.t[:, 0:nco], in0=bufA.t[:, c, 0:nco], scalar1=V(V_OMM + mi, c), scalar2=0.0, op0=ALU.mult, op1=ALU.add)
                        e.scalar_tensor_tensor(out=dst_view[:, c, 1:TT], in0=bufA.t[:, c, 0:TT - 1], scalar=V(V_MIX + mi, c),
                                               in1=tmpc.t[:, 1:TT], op0=ALU.mult, op1=ALU.add)
                        ins = e.scalar_tensor_tensor(out=dst_view[:, c, 0:1], in0=shcarry.t[:, c, :], scalar=V(V_MIX + mi, c),
                                                     in1=tmpc.t[:, 0:1], op0=ALU.mult, op1=ALU.add)
                        if tile == 0:
                            ins = e.scalar_tensor_tensor(out=dst_view[:, c, TT:NCOL], in0=shiftT.t[:, c, :], scalar=V(V_MIX + mi, c),
                                                         in1=tmpc.t[:, TT:NCOL], op0=ALU.mult, op1=ALU.add)
                    return ins
                S.op("dve", f, reads=[bufA, shcarry, shiftT, vecs, tmpc], writes=[tmpc, dst_tb])

            def lora_stage1(mi, wmat, nout, slot, func):
                make_mix(hb.t, hb, mi)
                for q0 in range(0, nout, 128):
                    qn = min(128, nout - q0)
                    wv, wb = load_w(wmat[:, q0:q0 + qn].rearrange("(k p) n -> p k n", p=128), 128, 16, qn)
                    for si, (c0, c1) in enumerate(segs):
                        bk = S.bank()
                        mm_acc(bk, qn, wv, hb.t, 128, KC, c0, c1, reads=[wb, hb])
                        S.op("act", lambda e, bk=bk, q0=q0, qn=qn, c0=c0, c1=c1: e.activation(
                            out=lora.t[0:qn, slot + q0 // 128, c0:c1], in_=bk.t[0:qn, 0:c1 - c0], func=func),
                            reads=[bk], writes=[lora])
            lora_stage1(1, w1, 96, 0, AF.Tanh)
            lora_stage1(4, a1, 96, 1, AF.Copy)
            lora_stage1(5, g1, 256, 2, AF.Sigmoid)
            make_mix(mixv[0], big, 0)
            make_mix(mixv[1], big, 2)
            make_mix(mixv[2], big, 3)
            S.op("dve", lambda e: e.tensor_copy(out=shcarry.t[:, :, :], in_=bufA.t[:, :, TT - 1:TT]), reads=[bufA], writes=[shcarry])

            def P_steps(hp):
                par = hp % 2
                cs = slice(hp * 128, (hp + 1) * 128)
                specs = [
                    (wsrc(w_r, 0, 16, hp * 128, 128), 128, 16, mixv[0], big, R_, tR, AF.Copy, None),
                    (wsrc(w_k, 0, 16, hp * 128, 128), 128, 16, mixv[1], big, K_, tK, AF.Copy, None),
                    (wsrc(w_v, 0, 16, hp * 128, 128), 128, 16, mixv[2], big, Vv[par], tVv[par], AF.Copy, None),
                    (w2[:, cs].rearrange("(k p) n -> p k n", p=96), 96, 1, lora.t[:, 0:1, :], lora, LW_, tLW, AF.Sigmoid, V(V_W0, hp)),
                    (a2[:, cs].rearrange("(k p) n -> p k n", p=96), 96, 1, lora.t[:, 1:2, :], lora, A_, tA, AF.Sigmoid, V(V_A0, hp)),
                    (g2[:, cs].rearrange("(k p) n -> p k n", p=128), 128, 2, lora.t[:, 2:4, :], lora, Gv[par], tGv[par], AF.Copy, None),
                ]

                def mm_ev(spec, wv, wb):
                    (src, kp, kchunks, act_view, act_tb, dstv, dsttb, func, bias) = spec
                    for si, (c0, c1) in enumerate(segs):
                        bk = S.bank()
                        mm_acc(bk, 128, wv, act_view, kp, kchunks, c0, c1, reads=[wb, act_tb])
                        if bias is None:
                            S.op("act", lambda e, bk=bk, c0=c0, c1=c1: e.activation(out=dstv[:, c0:c1], in_=bk.t[:, 0:c1 - c0], func=func),
                                 reads=[bk], writes=[dsttb])
                        else:
                            S.op("act", lambda e, bk=bk, c0=c0, c1=c1: e.activation(out=dstv[:, c0:c1], in_=bk.t[:, 0:c1 - c0], func=func,
                                                                                   bias=bias),
                                 reads=[bk, vecs], writes=[dsttb])
                n_sp = len(specs)
                h = load_w_dma(specs[0][0], specs[0][1], specs[0][2], 128)
                yield
                wprev = None
                for i in range(n_sp):
                    wcur = load_w_cast(h)
                    if i + 1 < n_sp:
                        nx = specs[i + 1]
                        h = load_w_dma(nx[0], nx[1], nx[2], 128)
                    if wprev is not None:
                        mm_ev(specs[i - 1], wprev[0], wprev[1])
                    wprev = wcur
                    yield
                mm_ev(specs[n_sp - 1], wprev[0], wprev[1])
                yield


            def bd_sum(src_view, src_tb, cb):
                for si, (c0, c1) in enumerate(segs):
                    bk = S.bank()
                    S.op("pe", lambda e, bk=bk, c0=c0, c1=c1: e.matmul(bk.t[:, 0:c1 - c0], BDm, src_view[:, c0:c1], start=True, stop=True),
                         reads=[src_tb, cst], writes=[bk])
                    cb(bk, c0, c1)

            def dve(fn, reads=(), writes=()):
                S.op("dve", fn, reads=list(reads), writes=list(writes))

            def act(fn, reads=(), writes=()):
                S.op("act", fn, reads=list(reads), writes=list(writes))

            def Q_gen(hp):
                par = hp % 2
                V_, tV, G_, tG = Vv[par], tVv[par], Gv[par], tGv[par]
                n_ = nco
                dve(lambda e: e.tensor_scalar(out=LW_[:, 0:n_], in0=LW_[:, 0:n_], scalar1=-0.6065306597126334, scalar2=0.0, op0=ALU.mult, op1=ALU.add),
                    reads=[tLW], writes=[tLW])
                yield
                dve(lambda e: e.tensor_scalar(out=KK_[:, 0:n_], in0=K_[:, 0:n_], scalar1=V(V_KK, hp), scalar2=0.0, op0=ALU.mult, op1=ALU.add),
                    reads=[tK, vecs], writes=[tKK])
                yield
                dve(lambda e: e.tensor_tensor(out=T1_[:, 0:n_], in0=KK_[:, 0:n_], in1=KK_[:, 0:n_], op=ALU.mult), reads=[tKK], writes=[tT1])
                yield

                def kkn(bk, c0, c1):
                    S.op("dve", lambda e: e.tensor_scalar(out=T2_[:, c0:c1], in0=bk.t[:, 0:c1 - c0], scalar1=1e-24, scalar2=0.0, op0=ALU.max, op1=ALU.add),
                         reads=[bk], writes=[tT2])
                bd_sum(T1_, tT1, kkn)
                yield
                act(lambda e: e.activation(out=T2_[:, 0:n_], in_=T2_[:, 0:n_], func=AF.Sqrt), reads=[tT2], writes=[tT2])
                yield
                dve(lambda e: e.reciprocal(out=T2_[:, 0:n_], in_=T2_[:, 0:n_]), reads=[tT2], writes=[tT2])
                yield
                dve(lambda e: e.tensor_tensor(out=KK_[:, 0:n_], in0=KK_[:, 0:n_], in1=T2_[:, 0:n_], op=ALU.mult), reads=[tKK, tT2], writes=[tKK])
                yield
                dve(lambda e: e.tensor_scalar(out=T1_[:, 0:n_], in0=A_[:, 0:n_], scalar1=-1.0, scalar2=V(V_KA, hp), op0=ALU.add, op1=ALU.mult),
                    reads=[tA, vecs], writes=[tT1])
                yield
                dve(lambda e: e.scalar_tensor_tensor(out=K_[:, 0:n_], in0=T1_[:, 0:n_], scalar=1.0, in1=K_[:, 0:n_], op0=ALU.add, op1=ALU.mult),
                    reads=[tT1, tK], writes=[tK])
                yield
                dve(lambda e: e.tensor_tensor(out=A_[:, 0:n_], in0=KK_[:, 0:n_], in1=A_[:, 0:n_], op=ALU.mult), reads=[tKK, tA], writes=[tA])
                yield
                dve(lambda e: e.scalar_tensor_tensor(out=T1_[:, 0:n_], in0=R_[:, 0:n_], scalar=V(V_RK, hp), in1=K_[:, 0:n_],
                                                     op0=ALU.mult, op1=ALU.mult), reads=[tR, tK, vecs], writes=[tT1])
                yield

                def bon(bk, c0, c1):
                    S.op("dve", lambda e: e.tensor_tensor(out=BON_[:, c0:c1], in0=V_[:, c0:c1], in1=bk.t[:, 0:c1 - c0], op=ALU.mult),
                         reads=[bk, tV], writes=[tBON])
                bd_sum(T1_, tT1, bon)
                yield

                for ch in range(TT // CH):
                    a0, a1_ = ch * CH, (ch + 1) * CH
                    dve(lambda e, a0=a0, a1_=a1_: e.tensor_tensor_scan(out=CW_[:, a0:a1_], data0=onesf.t[:, 0:CH], data1=LW_[:, a0:a1_],
                                                                      initial=0.0, op0=ALU.mult, op1=ALU.add), reads=[onesf, tLW], writes=[tCW])
                act(lambda e: e.activation(out=E_[:, 0:TT], in_=CW_[:, 0:TT], func=AF.Exp), reads=[tCW], writes=[tE])
                yield
                dve(lambda e: e.tensor_tensor(out=T3_[:, 0:TT], in0=R_[:, 0:TT], in1=E_[:, 0:TT], op=ALU.mult), reads=[tR, tE], writes=[tT3])
                yield
                dve(lambda e: e.tensor_copy(out=wcT.t[:, :], in_=E_[:, 0:TT].rearrange("p (c t) -> p c t", t=CH)[:, :, CH - 1]),
                    reads=[tE], writes=[wcT])
                yield
                act(lambda e: e.activation(out=E_[:, 0:TT], in_=CW_[:, 0:TT], func=AF.Exp, scale=-1.0), reads=[tCW], writes=[tE])
                yield
                dve(lambda e: e.tensor_tensor(out=T1_[:, 0:TT], in0=A_[:, 0:TT], in1=E_[:, 0:TT], op=ALU.mult), reads=[tA, tE], writes=[tT1])
                yield
                dve(lambda e: e.tensor_tensor(out=T2_[:, 0:TT], in0=K_[:, 0:TT], in1=E_[:, 0:TT], op=ALU.mult), reads=[tK, tE], writes=[tT2])
                yield
                dve(lambda e: e.tensor_tensor(out=E_[:, 0:TT], in0=CW_[:, 0:TT], in1=LW_[:, 0:TT], op=ALU.subtract), reads=[tCW, tLW], writes=[tE])
                yield
                act(lambda e: e.activation(out=E_[:, 0:TT], in_=E_[:, 0:TT], func=AF.Exp), reads=[tE], writes=[tE])
                yield
                dve(lambda e: e.scalar_tensor_tensor(out=E_[:, 0:TT], in0=KK_[:, 0:TT], scalar=-1.0, in1=E_[:, 0:TT],
                                                     op0=ALU.mult, op1=ALU.mult), reads=[tKK, tE], writes=[tE])
                yield
                AT_, BT_, KT_, RT_ = E_, T1_, T2_, T3_

                yield

            def mid(hp):
                par = hp % 2
                V_, tV, G_, tG = Vv[par], tVv[par], Gv[par], tGv[par]
                n_ = nco
                AT_, BT_, KT_, RT_ = E_, T1_, T2_, T3_
                if tile == 0:
                    def sample_half(half):
                        n0 = half * 8
                        sc0_, sc1_ = TT + n0, TT + n0 + 8
                        src = swkv[n0:n0 + 8, 2 * hp:2 * hp + 2, :, :].rearrange("n h v k -> (h v) n k")
                        dst = nws[n0:n0 + 8, 2 * hp:2 * hp + 2, :, :].rearrange("n h v k -> (h v) n k")
                        S.dma(SIN, src, writes=[xin], sembuf=xin)

                        def bcast(srcv, srctb, cb):
                            S.op("dve", lambda e: e.tensor_tensor(out=DX, in0=srcv.unsqueeze(2).broadcast_to([128, 8, 64]),
                                                                  in1=ident2.unsqueeze(1).broadcast_to([128, 8, 64]), op=ALU.mult),
                                 reads=[srctb, cst, xin], writes=[xin])
                            bk = S.bank()
                            S.op("pe", lambda e, bk=bk: e.matmul(bk.t[:, 0:512], BDm, DX.rearrange("p n k -> p (n k)"), start=True, stop=True),
                                 reads=[xin, cst], writes=[bk])
                            cb(bk.t[:, 0:512].rearrange("p (n k) -> p n k", k=64), bk)

                        def c_a(pv, bk):
                            def f(e):
                                e.tensor_tensor(out=TMS, in0=SIN, in1=pv, op=ALU.mult)
                                return e.tensor_reduce(out=SA, in_=TMS, axis=AX.X, op=ALU.add, negate=True)
                            S.op("dve", f, reads=[bk, xin], writes=[xin])
                        bcast(KK_[:, sc0_:sc1_], tKK, c_a)

                        def c_w(pv, bk):
                            S.op("act", lambda e: e.activation(out=TMS, in_=pv, func=AF.Exp), reads=[bk, xin], writes=[xin])
                            S.op("dve", lambda e: e.tensor_tensor(out=SIN, in0=SIN, in1=TMS, op=ALU.mult), reads=[xin], writes=[xin])
                        bcast(LW_[:, sc0_:sc1_], tLW, c_w)

                        def c_b(pv, bk):
                            def f(e):
                                e.tensor_tensor(out=TMS, in0=pv, in1=SA.unsqueeze(2).broadcast_to([128, 8, 64]), op=ALU.mult)
                                return e.tensor_tensor(out=SIN, in0=SIN, in1=TMS, op=ALU.add)
                            S.op("dve", f, reads=[bk, xin], writes=[xin])
                        bcast(A_[:, sc0_:sc1_], tA, c_b)

                        def c_k(pv, bk):
                            def f(e):
                                e.tensor_tensor(out=TMS, in0=pv, in1=V_[:, sc0_:sc1_].unsqueeze(2).broadcast_to([128, 8, 64]), op=ALU.mult)
                                return e.tensor_tensor(out=SIN, in0=SIN, in1=TMS, op=ALU.add)
                            S.op("dve", f, reads=[bk, xin, tV], writes=[xin])
                        bcast(K_[:, sc0_:sc1_], tK, c_k)

                        def c_r(pv, bk):
                            def f(e):
                                e.tensor_tensor(out=TMS, in0=SIN, in1=pv, op=ALU.mult)
                                return e.tensor_reduce(out=Y_[:, sc0_:sc1_], in_=TMS, axis=AX.X, op=ALU.add)
                            S.op("dve", f, reads=[bk, xin], writes=[xin, tY])
                        bcast(R_[:, sc0_:sc1_], tR, c_r)
                        S.dma(dst, SIN, reads=[xin], sembuf=xin)

                    for half_ in range(2):
                        sample_half(half_)

                def fcast(e):
                    e.tensor_copy(out=AB3.t[:, 0:TT], in_=AT_[:, 0:TT])
                    e.tensor_copy(out=AB3.t[:, TT:2 * TT], in_=BT_[:, 0:TT])
                    return e.tensor_copy(out=AB3.t[:, 2 * TT:3 * TT], in_=RT_[:, 0:TT])
                S.op("pool", fcast, reads=[tE, tT1, tT3], writes=[AB3])
                S.op("act", lambda e: e.activation(out=S0Tb.t[:, :], in_=S0T.t[:, hp, :], func=AF.Copy), reads=[S0T], writes=[S0Tb])

                def stageA(ch):
                    p = ch % 3
                    a0, a1_ = ch * CH, (ch + 1) * CH
                    ex, tk, ms, xw, xwb, tat = EX[p], TK[p], MS[p], XW[p], XWB[p], TAT[p]
                    ATx, BTx, KTx, RTx, Vx = (ex.t[:, i * 128:(i + 1) * 128] for i in range(5))
                    N_bd, AakT_bd, A_bd, ArbT_bd, ArkT_bd = (ms.t[:, i * 128:(i + 1) * 128] for i in range(5))

                    def bc2(v, off=0):
                        return v[:, off + a0:off + a1_].unsqueeze(1).broadcast_to([128, 2, CH])
                    bdv = BDm.rearrange("p (h t) -> p h t", h=2)

                    def fex(e):
                        for i, src in enumerate((AT_, BT_, KT_, RT_, V_)):
                            ins = e.tensor_tensor(out=ex.t[:, i * 128:(i + 1) * 128].rearrange("p (h t) -> p h t", h=2), in0=bc2(src), in1=bdv,
                                                  op=ALU.mult)
                        return ins
                    S.op("pool", fex, reads=[tE, tT1, tT2, tT3, tV, cst], writes=[ex])
                    bkt = S.bank()

                    def ftr(e):
                        e.matmul(bkt.t[:, 0:128], BTx, identb.t[:, 0:128], start=True, stop=True)
                        e.matmul(bkt.t[:, 128:256], KTx, identb.t[:, 0:128], start=True, stop=True)
                        return e.matmul(bkt.t[:, 256:320], Vx, identb.t[:, 128:192], start=True, stop=True)
                    S.op("pe", ftr, reads=[ex, identb], writes=[bkt])
                    S.op("act", lambda e: e.activation(out=tk.t[:, :], in_=bkt.t[:, 0:320], func=AF.Copy), reads=[bkt], writes=[tk])
                    bka = S.bank()
                    bkb = S.bank()
                    at2, bt2, rt2 = bc2(AB3.t, 0), bc2(AB3.t, TT), bc2(AB3.t, 2 * TT)

                    def fA(e):
                        e.matmul(bka.t[:, 0:128], BTx, at2, start=True, stop=True)
                        e.matmul(bka.t[:, 128:256], KTx, at2, start=True, stop=True)
                        e.matmul(bka.t[:, 256:384], ATx, bt2, start=True, stop=True)
                        e.matmul(bkb.t[:, 0:128], BTx, rt2, start=True, stop=True)
                        return e.matmul(bkb.t[:, 128:256], KTx, rt2, start=True, stop=True)
                    S.op("pe", fA, reads=[ex, AB3], writes=[bka, bkb])

                    def fm(e):
                        e.tensor_tensor(out=ms.t[:, 0:256].rearrange("p (a n) -> p a n", a=2), in0=bka.t[:, 0:256].rearrange("p (a n) -> p a n", a=2),
                                        in1=msu_bd.unsqueeze(1).broadcast_to([128, 2, 128]), op=ALU.mult)
                        e.tensor_tensor(out=A_bd, in0=bka.t[:, 256:384], in1=msl_bd, op=ALU.mult)
                        return e.tensor_tensor(out=ms.t[:, 384:640].rearrange("p (a n) -> p a n", a=2),
                                               in0=bkb.t[:, 0:256].rearrange("p (a n) -> p a n", a=2),
                                               in1=msi_bd.unsqueeze(1).broadcast_to([128, 2, 128]), op=ALU.mult)
                    S.op("dve", fm, reads=[bka, bkb, cst], writes=[ms])
                    yield
                    bkr = S.bank()

                    def fr(e):
                        e.matmul(bkr.t[:, 0:64], AakT_bd, tk.t[:, 256:320], start=True, stop=True)
                        return e.matmul(bkr.t[:, 64:192], ATx, identb.t[:, 0:128], start=True, stop=True)
                    S.op("pe", fr, reads=[ms, tk, ex, identb], writes=[bkr])
                    S.op("dve", lambda e: e.tensor_copy(out=xwb.t[:, :], in_=bkr.t[:, 0:192]), reads=[bkr], writes=[xwb])
                    S.op("dve", lambda e: e.tensor_copy(out=xw.t[:, :], in_=bkr.t[:, 0:192]), reads=[bkr], writes=[xw])
                    yield
                    Pc, Mc, cur = A_bd, N_bd, ms
                    acc = S.banks[5 + p]
                    for j in range(6):
                        S.op("pe", lambda e, Mc=Mc, j=j: e.matmul(acc.t[:, 0:192], Mc, xwb.t[:, :], start=(j == 0), stop=(j == 5)),
                             reads=[cur, xwb], writes=[acc])
                        if j < 5:
                            bkp = S.bank()
                            nxt = PMA[p] if j % 2 == 0 else PMB[p]

                            def fsq(e, bkp=bkp, Mc=Mc, Pc=Pc):
                                e.matmul(bkp.t[:, 0:128], Mc, Pc, start=True, stop=True)
                                return e.matmul(bkp.t[:, 128:256], Pc, Mc, start=True, stop=True)
                            S.op("pe", fsq, reads=[cur], writes=[bkp])
                        S.op("dve", lambda e: e.tensor_tensor(out=xwb.t[:, :], in0=xw.t[:, :], in1=acc.t[:, 0:192], op=ALU.add),
                             reads=[acc, xw], writes=[xwb])
                        if j < 5:
                            S.op("act", lambda e, bkp=bkp, nxt=nxt: e.activation(out=nxt.t[:, :], in_=bkp.t[:, 0:256], func=AF.Copy),
                                 reads=[bkp], writes=[nxt])
                            Pc, Mc, cur = nxt.t[:, 0:128], nxt.t[:, 128:256], nxt
                        yield
                    S.op("dve", lambda e: e.tensor_tensor(out=xw.t[:, 0:64], in0=xw.t[:, 0:64], in1=acc.t[:, 0:64], op=ALU.add),
                         reads=[acc, xw], writes=[xw])
                    bkq = S.bank()
                    S.op("pe", lambda e: e.matmul(bkq.t[:, 0:128], xwb.t[:, 64:192], identb.t[:, 0:128], start=True, stop=True),
                         reads=[xwb, identb], writes=[bkq])
                    S.op("act", lambda e: e.activation(out=tat.t[:, :], in_=bkq.t[:, 0:128], func=AF.Copy), reads=[bkq], writes=[tat])
                    yield

                def stageB(ch):
                    p = ch % 3
                    a0, a1_ = ch * CH, (ch + 1) * CH
                    ex, tk, ms, xw, tat, uu, yx = EX[p], TK[p], MS[p], XW[p], TAT[p], UU[ch % 2], YX[ch % 2]
                    RTx = ex.t[:, 384:512]
                    ArbT_bd, ArkT_bd = ms.t[:, 384:512], ms.t[:, 512:640]
                    s0 = S0T.t[:, hp, :]
                    bku = S.bank()
                    S.op("pe", lambda e: e.matmul(bku.t[:, 0:64], tat.t[:, :], S0Tb.t[:, :], start=True, stop=True), reads=[tat, S0Tb], writes=[bku])
                    S.op("dve", lambda e: e.tensor_tensor(out=uu.t[:, :], in0=xw.t[:, 0:64], in1=bku.t[:, 0:64], op=ALU.add),
                         reads=[bku, xw], writes=[uu])
                    S.op("dve", lambda e: e.tensor_scalar(out=s0, in0=s0, scalar1=wcT.t[:, ch:ch + 1], scalar2=0.0, op0=ALU.mult, op1=ALU.add),
                         reads=[S0T, wcT], writes=[S0T])
                    yield
                    bky = S.bank()
                    bks = S.bank()

                    def fy(e):
                        e.matmul(bky.t[:, 0:64], RTx, S0Tb.t[:, :], start=True, stop=False)
                        e.matmul(bky.t[:, 0:64], ArbT_bd, uu.t[:, :], start=False, stop=False)
                        return e.matmul(bky.t[:, 0:64], ArkT_bd, tk.t[:, 256:320], start=False, stop=True)
                    S.op("pe", fy, reads=[ex, S0Tb, ms, uu, tk], writes=[bky])

                    def fs(e):
                        e.matmul(bks.t[:, 0:64], tk.t[:, 0:128], uu.t[:, :], start=True, stop=False)
                        return e.matmul(bks.t[:, 0:64], tk.t[:, 128:256], tk.t[:, 256:320], start=False, stop=True)
                    S.op("pe", fs, reads=[tk, uu], writes=[bks])
                    S.op("dve", lambda e: e.scalar_tensor_tensor(out=S0Tb.t[:, :], in0=bks.t[:, 0:64], scalar=wcT.t[:, ch:ch + 1], in1=s0,
                                                                 op0=ALU.mult, op1=ALU.add), reads=[bks, wcT, S0T], writes=[S0Tb])
                    S.op("dve", lambda e: e.scalar_tensor_tensor(out=s0, in0=bks.t[:, 0:64], scalar=wcT.t[:, ch:ch + 1], in1=s0,
                                                                 op0=ALU.mult, op1=ALU.add), reads=[bks, wcT, S0T], writes=[S0T])
                    S.op("dve", lambda e: e.tensor_tensor(out=yx.t[:, :].rearrange("p (h v) -> p h v", h=2),
                                                          in0=bky.t[:, 0:64].unsqueeze(1).broadcast_to([128, 2, 64]),
                                                          in1=BDm.rearrange("p (h t) -> p h t", h=2), op=ALU.mult),
                         reads=[bky, cst], writes=[yx])
                    yield
                    bkz = S.bank()
                    S.op("pe", lambda e: e.matmul(bkz.t[:, 0:64], yx.t[:, :], ident2, start=True, stop=True), reads=[yx, cst], writes=[bkz])
                    S.op("act", lambda e: e.activation(out=Y_[:, a0:a1_], in_=bkz.t[:, 0:64], func=AF.Copy), reads=[bkz], writes=[tY])
                    yield

                NCH = TT // CH
                active = []
                nextA, nextB, doneA, doneB = 0, 0, set(), 0
                pg = P_steps(hp + 1) if hp + 1 < KC else None
                tick = 0
                while doneB < NCH:
                    tick += 1
                    if pg is not None and tick % 4 == 0:
                        try:
                            next(pg)
                        except StopIteration:
                            pg = None
                    nA = sum(1 for a in active if a[0] == "A")
                    if nextA < NCH and nA < 2 and nextA <= doneB + 2:
                        active.append(["A", nextA, stageA(nextA)])
                        nextA += 1
                    if nextB < NCH and nextB in doneA and nextB == doneB and not any(a[0] == "B" for a in active):
                        active.append(["B", nextB, stageB(nextB)])
                        nextB += 1
                    for a in list(active):
                        try:
                            next(a[2])
                        except StopIteration:
                            active.remove(a)
                            if a[0] == "A":
                                doneA.add(a[1])
                            else:
                                doneB += 1
                if pg is not None:
                    for _ in pg:
                        pass

            def N_gen(hp):
                par = hp % 2
                G_, tG = Gv[par], tGv[par]
                n_ = nco

                def gn1(bk, c0, c1):
                    S.op("dve", lambda e: e.scalar_tensor_tensor(out=Y_[:, c0:c1], in0=bk.t[:, 0:c1 - c0], scalar=-1.0 / 64, in1=Y_[:, c0:c1],
                                                                 op0=ALU.mult, op1=ALU.add), reads=[bk, tY], writes=[tY])
                bd_sum(Y_, tY, gn1)
                yield
                dve(lambda e: e.tensor_tensor(out=CW_[:, 0:n_], in0=Y_[:, 0:n_], in1=Y_[:, 0:n_], op=ALU.mult), reads=[tY], writes=[tCW])
                yield

                def gn2(bk, c0, c1):
                    S.op("act", lambda e: e.activation(out=CW_[:, c0:c1], in_=bk.t[:, 0:c1 - c0], func=AF.Sqrt, bias=epsb.t[:, 1:2], scale=1.0 / 64),
                         reads=[bk, epsb], writes=[tCW])
                bd_sum(CW_, tCW, gn2)
                yield
                dve(lambda e: e.reciprocal(out=CW_[:, 0:n_], in_=CW_[:, 0:n_]), reads=[tCW], writes=[tCW])
                yield
                dve(lambda e: e.tensor_tensor(out=Y_[:, 0:n_], in0=Y_[:, 0:n_], in1=CW_[:, 0:n_], op=ALU.mult), reads=[tY, tCW], writes=[tY])
                yield
                dve(lambda e: e.tensor_scalar(out=Y_[:, 0:n_], in0=Y_[:, 0:n_], scalar1=V(V_LNW, hp), scalar2=V(V_LNB, hp),
                                              op0=ALU.mult, op1=ALU.add), reads=[tY, vecs], writes=[tY])
                yield
                dve(lambda e: e.tensor_tensor(out=Y_[:, 0:n_], in0=Y_[:, 0:n_], in1=BON_[:, 0:n_], op=ALU.add), reads=[tY, tBON], writes=[tY])
                yield
                dve(lambda e: e.tensor_tensor(out=hb.t[:, hp, 0:n_], in0=Y_[:, 0:n_], in1=G_[:, 0:n_], op=ALU.mult), reads=[tY, tG], writes=[hb])
                yield

            def rr(gens):
                alive = list(gens)
                while alive:
                    for g in list(alive):
                        try:
                            next(g)
                        except StopIteration:
                            alive.remove(g)

            for _ in P_steps(0):
                pass
            rr([Q_gen(0)])
            for hp_ in range(KC):
                mid(hp_)
                rr([N_gen(hp_)] + ([Q_gen(hp_ + 1)] if hp_ + 1 < KC else []))
            lin(w_o, KC, D, hb.t, hb, evac_to(bufA), tile)
            resid_add(bufA, V_MIXPOST + 1, tile)
            ffn(1)

            for blk in range(4):
                store_T(yp[t0 + blk * 128:t0 + (blk + 1) * 128, :], lambda c, blk=blk: x.t[:, c, blk * 128:(blk + 1) * 128], 128, x)
            if tile == 0:
                store_T(ysd, lambda c: x.t[:, c, TT:NCOL], NS, x)

        for tile_ in range(n_tiles):
            do_tile(tile_)

        for hp in range(KC):
            bk = S.bank()
            S.op("pe", lambda e, bk=bk, hp=hp: e.transpose(bk.t[0:64, 0:128], S0T.t[:, hp, :], ident), reads=[S0T, cst], writes=[bk])
            S.op("act", lambda e, bk=bk, hp=hp: e.activation(out=xin.t[0:64, hp * 128:(hp + 1) * 128], in_=bk.t[0:64, 0:128], func=AF.Copy),
                 reads=[bk], writes=[xin])
        S.dma(nwp.rearrange("(c h) v k -> v c h k", h=2), xin.t[0:64, :].rearrange("v (c h k) -> v c h k", h=2, k=64), reads=[xin], sembuf=xin)

        allt = [x, bufA, hb, big, xin, vecs, cst] + wst + wbf
        final_waits = []
        for tb in allt:
            if tb.dsem is not None:
                final_waits.append((tb.dsem, 16 * tb.ndma))
        with nc.Block() as block:
            S.emit(block, final_waits)
    return nc


def _pack_vec(v):
    return np.ascontiguousarray(np.asarray(v, np.float32).reshape(KC, 128).T)


def _consts():
    c = np.zeros((128, NCST), np.float32)
    c[:, 0:128] = np.eye(128)
    c[0:64, 128:192] = 1.0
    c[64:128, 192:256] = 1.0
    su = np.triu(np.ones((64, 64), np.float32), 1)
    c[0:64, 256:320] = su
    c[0:64, 320:384] = np.triu(np.ones((64, 64), np.float32), 0)
    c[0:64, 384:448] = su.T
    c[0:64, 448 + 64:448 + 128] = np.eye(64)
    c[0:64, 576:640] = np.eye(64)
    c[64:128, 576:640] = np.eye(64)
    si = np.triu(np.ones((64, 64), np.float32), 0)
    for h in range(2):
        sl_ = slice(64 * h, 64 * h + 64)
        c[sl_, 640 + 64 * h:640 + 64 * h + 64] = su
        c[sl_, 768 + 64 * h:768 + 64 * h + 64] = si
        c[sl_, 896 + 64 * h:896 + 64 * h + 64] = su.T
    return c


_NC_CACHE = {}


def kernel(x_prompt, x_sample, cache_conv, state_wkv, state_shift,
           norm_mix_pre, norm_mix_post, norm_ffn_pre, norm_ffn_post,
           conv_w_in, conv_w, conv_w_out,
           rwkv_mix, rwkv_w_r, rwkv_w_k, rwkv_w_v, rwkv_w_o,
           rwkv_w0, rwkv_w1, rwkv_w2, rwkv_a0, rwkv_a1, rwkv_a2,
           rwkv_g1, rwkv_g2, rwkv_k_k, rwkv_k_a, rwkv_r_k, rwkv_ln_w, rwkv_ln_b,
           ffn_w_gate, ffn_w_up, ffn_w_down):
    f = lambda a: np.ascontiguousarray(np.asarray(a, np.float32))
    vl = [None] * NV
    for i in range(2):
        vl[V_MIXPRE + i] = norm_mix_pre[i]
        vl[V_MIXPOST + i] = norm_mix_post[i]
        vl[V_FFNPRE + i] = norm_ffn_pre[i]
        vl[V_FFNPOST + i] = norm_ffn_post[i]
    for i in range(3):
        vl[V_CONVW + i] = conv_w[0, i]
    for i in range(6):
        vl[V_MIX + i] = rwkv_mix[0, i]
    vl[V_W0] = rwkv_w0[0]
    vl[V_A0] = rwkv_a0[0]
    vl[V_KK] = rwkv_k_k[0]
    vl[V_KA] = rwkv_k_a[0]
    vl[V_RK] = np.asarray(rwkv_r_k[0]).reshape(-1)
    vl[V_LNW] = rwkv_ln_w[0]
    vl[V_LNB] = rwkv_ln_b[0]
    vecs = np.zeros((128, NV, KC), np.float32)
    for i, v in enumerate(vl):
        if v is not None:
            vecs[:, i, :] = _pack_vec(v)
    vecs = vecs.reshape(128, NV * KC)
    shared = {
        "vecs": vecs, "cst": _consts(),
        "w_in": f(conv_w_in[0]), "w_out": f(conv_w_out[0]),
        "w_r": f(rwkv_w_r[0]), "w_k": f(rwkv_w_k[0]), "w_v": f(rwkv_w_v[0]), "w_o": f(rwkv_w_o[0]),
        "w1": f(rwkv_w1[0]), "w2": f(rwkv_w2[0]), "a1": f(rwkv_a1[0]), "a2": f(rwkv_a2[0]),
        "g1": f(rwkv_g1[0]), "g2": f(rwkv_g2[0]),
        "wg0": f(ffn_w_gate[0]), "wg1": f(ffn_w_gate[1]),
        "wu0": f(ffn_w_up[0]), "wu1": f(ffn_w_up[1]),
        "wd0": f(ffn_w_down[0]), "wd1": f(ffn_w_down[1]),
    }
    in_maps = []
    for c in range(8):
        b = c % 4
        m = dict(shared)
        m["xp"] = f(x_prompt[b])
        m["xs"] = f(x_sample[c * NS:(c + 1) * NS, 0])
        m["cconv"] = f(cache_conv[0, c * NS:(c + 1) * NS]).reshape(NS * 2, D)
        m["swkv"] = f(state_wkv[0, c * NS:(c + 1) * NS])
        m["sshift"] = f(state_shift[0, c * NS:(c + 1) * NS])
        in_maps.append(m)
    if "nc" not in _NC_CACHE:
        _NC_CACHE["nc"] = build_program()
    res = run_bass_kernel_spmd(_NC_CACHE["nc"], in_maps, core_ids=list(range(8)))
    R = res.results
    y_prompt = np.stack([R[b]["yp"] for b in range(4)]).astype(np.float32)
    y_sample = np.concatenate([R[c]["ys"] for c in range(8)]).reshape(128, 1, D).astype(np.float32)
    conv_p = np.stack([R[b]["ncp"] for b in range(4)])[None].astype(np.float32)
    conv_s = np.concatenate([R[c]["ncs"].reshape(NS, 2, D) for c in range(8)])[None].astype(np.float32)
    wkv_p = np.stack([R[b]["nwp"] for b in range(4)])[None].astype(np.float32)
    wkv_s = np.concatenate([R[c]["nws"] for c in range(8)])[None].astype(np.float32)
    shift_p = np.stack([R[b]["nsp"].reshape(D) for b in range(4)])[None].astype(np.float32)
    shift_s = np.concatenate([R[c]["nss"] for c in range(8)])[None].astype(np.float32)
    return (y_prompt, y_sample, conv_p, conv_s, wkv_p, wkv_s, shift_p, shift_s)
```

```python
import numpy as np
from contextlib import ExitStack
import concourse.bass as bass
import concourse.mybir as mybir
from concourse.bass_utils import run_bass_kernel_spmd

F32 = mybir.dt.float32
BF16 = mybir.dt.bfloat16
ALU = mybir.AluOpType
AF = mybir.ActivationFunctionType
AX = mybir.AxisListType

D = 2048
KC = 16
DFF = 5632
FC = 44
TT = 512
NTILE = 4
NS = 16
NCOL = TT + NS
CH = 64
NV = 32
EPS = 1e-6
GN_EPS = 64 * 1e-5

V_MIXPRE, V_MIXPOST, V_FFNPRE, V_FFNPOST = 0, 2, 4, 6
V_CONVW = 8
V_MIX = 11
V_W0, V_A0, V_KK, V_KA, V_RK, V_LNW, V_LNB = 17, 18, 19, 20, 21, 22, 23
V_OMM = 24


class TB:
    def __init__(self, t, name):
        self.t = t
        self.name = name
        self.w = None
        self.r = {}
        self.dsem = None
        self.ndma = 0


class Sched:
    ENG = ["pe", "act", "dve", "pool", "sp"]

    def __init__(self, nc, es):
        self.nc = nc
        self.es = es
        self.prog = {e: [] for e in self.ENG}
        self.cnt = {e: 0 for e in self.ENG}
        self.sem = {e: es.enter_context(nc.semaphore("sem_" + e)) for e in self.ENG if e != "sp"}
        self.waited = {e: {} for e in self.ENG}
        self.banks = []
        self.bank_i = 0
        self.nsem = 0

    def sbuf(self, name, shape, dt=F32):
        t = self.es.enter_context(self.nc.sbuf_tensor("sb_" + name, list(shape), dt))
        return TB(t, name)

    def mkbanks(self):
        for i in range(8):
            t = self.es.enter_context(self.nc.psum_tensor("bank%d" % i, [128, 512], F32))
            self.banks.append(TB(t, "bank%d" % i))

    def bank(self):
        b = self.banks[self.bank_i % 8]
        self.bank_i += 1
        return b

    @staticmethod
    def _exp(bs):
        out = []
        for b in bs:
            out.append(b)
            out.extend(getattr(b, "aliases", ()))
        return out

    def _collect(self, eng, reads, writes, is_dma):
        evs = []
        for b in reads:
            if b.w is not None:
                evs.append(b.w + ("raw",))
        for b in writes:
            if b.w is not None:
                evs.append(b.w + ("waw",))
            for ev in b.r.values():
                evs.append(ev + ("war",))
        waits = {}
        for (key, sem, val, src, kind) in evs:
            if src == eng and not is_dma and key == "e_" + eng:
                if eng == "pe":
                    continue
                if kind != "raw":
                    continue
            if self.waited[eng].get(key, 0) >= val:
                continue
            if key not in waits or waits[key][1] < val:
                waits[key] = (sem, val)
        for key, (sem, val) in waits.items():
            self.waited[eng][key] = val
        return list(waits.values())

    def op(self, eng, fn, reads=(), writes=()):
        reads, writes = self._exp(reads), self._exp(writes)
        waits = self._collect(eng, reads, writes, False)
        self.cnt[eng] += 1
        ev = ("e_" + eng, self.sem[eng], self.cnt[eng], eng)
        self.prog[eng].append((waits, fn, (self.sem[eng], 1)))
        for b in reads:
            b.r[ev[0]] = ev
        for b in writes:
            b.w = ev
            b.r = {}

    def dma(self, out, in_, reads=(), writes=(), sembuf=None, eng="sp"):
        sb = sembuf
        reads, writes = self._exp(reads), self._exp(writes)
        if sb.dsem is None:
            sb.dsem = self.es.enter_context(self.nc.semaphore("dsem%d" % self.nsem))
            sb.dkey = "d%d" % self.nsem
            self.nsem += 1
        waits = self._collect(eng, reads, writes, True)
        if sb.ndma > 0 and self.waited[eng].get(sb.dkey, 0) < 16 * sb.ndma:
            waits.append((sb.dsem, 16 * sb.ndma))
            self.waited[eng][sb.dkey] = 16 * sb.ndma
        sb.ndma += 1
        ev = (sb.dkey, sb.dsem, 16 * sb.ndma, "dma")
        self.prog[eng].append((waits, lambda e: e.dma_start(out=out, in_=in_), (sb.dsem, 16)))
        for b in reads:
            b.r[ev[0]] = ev
        for b in writes:
            b.w = ev
            b.r = {}
        return ev

    def emit(self, block, final_waits):
        nc = self.nc

        def run(engname):
            def body(e):
                for (waits, fn, inc) in self.prog[engname]:
                    for (sem, val) in waits:
                        e.wait_ge(sem, val)
                    ins = fn(e)
                    ins.then_inc(inc[0], inc[1])
                if engname == "sp":
                    for (sem, val) in final_waits:
                        e.wait_ge(sem, val)
            return body

        block.tensor(run("pe"))
        block.scalar(run("act"))
        block.vector(run("dve"))
        block.gpsimd(run("pool"))
        block.sync(run("sp"))


NCST = 640 + 384


def build_program(n_tiles=NTILE):
    nc = bass.Bass("TRN2", target_bir_lowering=False)

    def din(name, shape):
        return nc.dram_tensor(name, list(shape), F32, kind="ExternalInput").ap()

    def dout(name, shape):
        return nc.dram_tensor(name, list(shape), F32, kind="ExternalOutput").ap()

    TSEQ = n_tiles * TT
    xp = din("xp", [TSEQ, D])
    xs = din("xs", [NS, D])
    cconv = din("cconv", [NS * 2, D])
    swkv = din("swkv", [NS, 32, 64, 64])
    sshift = din("sshift", [NS, D])
    vecs_d = din("vecs", [128, NV * KC])
    cst_d = din("cst", [128, NCST])
    w_in = din("w_in", [D, 3 * D])
    w_out = din("w_out", [D, D])
    w_r = din("w_r", [D, D])
    w_k = din("w_k", [D, D])
    w_v = din("w_v", [D, D])
    w_o = din("w_o", [D, D])
    w1 = din("w1", [D, 96])
    w2 = din("w2", [96, D])
    a1 = din("a1", [D, 96])
    a2 = din("a2", [96, D])
    g1 = din("g1", [D, 256])
    g2 = din("g2", [256, D])
    wg = [din("wg%d" % i, [D, DFF]) for i in range(2)]
    wu = [din("wu%d" % i, [D, DFF]) for i in range(2)]
    wd = [din("wd%d" % i, [DFF, D]) for i in range(2)]

    yp = dout("yp", [TSEQ, D])
    ysd = dout("ys", [NS, D])
    ncp = dout("ncp", [2, D])
    ncs = dout("ncs", [NS * 2, D])
    nwp = dout("nwp", [32, 64, 64])
    nws = dout("nws", [NS, 32, 64, 64])
    nsp = dout("nsp", [1, D])
    nss = dout("nss", [NS, D])

    with ExitStack() as es:
        S = Sched(nc, es)
        S.mkbanks()
        x = S.sbuf("x", [128, KC, NCOL])
        bufA = S.sbuf("bufA", [128, KC, NCOL])
        hb = S.sbuf("hb", [128, KC, NCOL], BF16)
        big = S.sbuf("big", [128, 3 * KC * NCOL], BF16)
        NWST, NWBF = 6, 2
        wst = [S.sbuf("wst%d" % i, [128, 4 * 128]) for i in range(NWST)]
        wbf = [S.sbuf("wbf%d" % i, [128, 16 * 128], BF16) for i in range(NWBF)]
        xin = S.sbuf("xin", [128, D])
        vecs = S.sbuf("vecs", [128, NV, KC])
        cst = S.sbuf("cst", [128, NCST])
        onesb = S.sbuf("onesb", [128, 128], BF16)
        onesf = S.sbuf("onesf", [128, 64])
        epsb = S.sbuf("epsb", [128, 2])
        rstd = S.sbuf("rstd", [128, NCOL])
        tmpc = S.sbuf("tmpc", [128, NCOL + 2])
        tmpd = S.sbuf("tmpd", [128, NCOL + 2])
        ucarry = S.sbuf("ucarry", [128, KC, 2])
        cacheT = S.sbuf("cacheT", [128, KC, 2 * NS])
        newcs = S.sbuf("newcs", [128, KC, 2 * NS])
        shcarry = S.sbuf("shcarry", [128, KC, 1])
        shiftT = S.sbuf("shiftT", [128, KC, NS])
        S0T = S.sbuf("S0T", [128, KC, 64])
        lora = S.sbuf("lora", [128, 4, NCOL], BF16)
        wcT = S.sbuf("wcT", [128, TT // CH])
        EX = [S.sbuf("EX%d" % i, [128, 640], BF16) for i in range(3)]
        TK = [S.sbuf("TK%d" % i, [128, 320], BF16) for i in range(3)]
        MS = [S.sbuf("MS%d" % i, [128, 640], BF16) for i in range(3)]
        PMA = [S.sbuf("PMA%d" % i, [128, 256], BF16) for i in range(3)]
        PMB = [S.sbuf("PMB%d" % i, [128, 256], BF16) for i in range(3)]
        XW = [S.sbuf("XW%d" % i, [128, 192]) for i in range(3)]
        XWB = [S.sbuf("XWB%d" % i, [128, 192], BF16) for i in range(3)]
        TAT = [S.sbuf("TAT%d" % i, [128, 128], BF16) for i in range(3)]
        UU = [S.sbuf("UU%d" % i, [128, 64], BF16) for i in range(2)]
        AB3 = S.sbuf("AB3", [128, 3 * TT], BF16)
        S0Tb = S.sbuf("S0Tb", [128, 64], BF16)
        identb = S.sbuf("identb", [128, 192], BF16)
        YX = [S.sbuf("YX%d" % i, [128, 128]) for i in range(2)]

        ident = cst.t[:, 0:128]
        BDm = cst.t[:, 128:256]
        m_su = cst.t[0:64, 256:320]
        m_si = cst.t[0:64, 320:384]
        m_sl = cst.t[0:64, 384:448]
        shid = cst.t[0:64, 448:576]
        ident2 = cst.t[:, 576:640]
        msu_bd = cst.t[:, 640:768]
        msi_bd = cst.t[:, 768:896]
        msl_bd = cst.t[:, 896:1024]

        def V(i, c):
            return vecs.t[:, i, c:c + 1]

        S.dma(vecs.t[:].rearrange("p v c -> p (v c)"), vecs_d, writes=[vecs], sembuf=vecs)
        S.dma(cst.t[:], cst_d, writes=[cst], sembuf=cst)
        S.op("dve", lambda e: e.memset(onesb.t[:], 1.0), writes=[onesb])
        S.op("dve", lambda e: e.tensor_copy(out=identb.t[:, 0:128], in_=cst.t[:, 0:128]), reads=[cst], writes=[identb])
        S.op("dve", lambda e: e.tensor_copy(out=identb.t[:, 128:192], in_=cst.t[:, 576:640]), reads=[cst], writes=[identb])
        S.op("dve", lambda e: e.memset(onesf.t[:], 1.0), writes=[onesf])
        S.op("dve", lambda e: e.memset(epsb.t[:, 0:1], EPS), writes=[epsb])
        S.op("dve", lambda e: e.memset(epsb.t[:, 1:2], GN_EPS), writes=[epsb])
        S.op("dve", lambda e: e.memset(ucarry.t[:], 0.0), writes=[ucarry])
        S.op("dve", lambda e: e.memset(shcarry.t[:], 0.0), writes=[shcarry])
        S.op("dve", lambda e: e.memset(S0T.t[:], 0.0), writes=[S0T])
        S.op("dve", lambda e: e.tensor_scalar(out=vecs.t[:, V_OMM:V_OMM + 6, :], in0=vecs.t[:, V_MIX:V_MIX + 6, :],
                                              scalar1=-1.0, scalar2=1.0, op0=ALU.mult, op1=ALU.add),
             reads=[vecs], writes=[vecs])

        def segs_of(tile):
            return [(0, TT)] + ([(TT, NCOL)] if tile == 0 else [])

        def ncols_of(tile):
            return NCOL if tile == 0 else TT

        wctr = [0, 0]

        def load_w_dma(src3, kp, kcn, n, engs=("pool", "act")):
            parts = []
            k0 = 0
            while k0 < kcn:
                kk_ = min(4, kcn - k0)
                st = wst[wctr[1] % NWST]
                eng = engs[wctr[1] % 2]
                wctr[1] += 1
                sv = st.t[0:kp, 0:kk_ * n].rearrange("p (k n) -> p k n", n=n)
                S.dma(sv, src3[:, k0:k0 + kk_, :], writes=[st], sembuf=st)
                parts.append((st, sv, k0, kk_, eng))
                k0 += kk_
            return (parts, kp, kcn, n)

        def load_w_cast(h):
            parts, kp, kcn, n = h
            wb = wbf[wctr[0] % NWBF]
            wctr[0] += 1
            dstb = wb.t[0:kp, 0:kcn * n].rearrange("p (k n) -> p k n", n=n)
            for (st, sv, k0, kk_, eng) in parts:
                if eng in ("pool", "dve"):
                    S.op(eng, lambda e, a=dstb[:, k0:k0 + kk_, :], b=sv: e.tensor_copy(out=a, in_=b),
                         reads=[st], writes=[wb])
                else:
                    S.op("act", lambda e, a=dstb[:, k0:k0 + kk_, :], b=sv: e.activation(out=a, in_=b, func=AF.Copy),
                         reads=[st], writes=[wb])
            return dstb, wb

        def load_w(src3, kp, kcn, n):
            return load_w_cast(load_w_dma(src3, kp, kcn, n))

        DENSE_CAST = ("dve", "act")

        def wstream(specs):
            n = len(specs)
            st = {"i": 0, "h": {}, "w": {}}
            st["h"][0] = load_w_dma(*specs[0], engs=DENSE_CAST)
            st["w"][0] = load_w_cast(st["h"].pop(0))
            if n > 1:
                st["h"][1] = load_w_dma(*specs[1], engs=DENSE_CAST)

            def get():
                i = st["i"]
                st["i"] += 1
                if i + 1 < n:
                    st["w"][i + 1] = load_w_cast(st["h"].pop(i + 1))
                if i + 2 < n:
                    st["h"][i + 2] = load_w_dma(*specs[i + 2], engs=DENSE_CAST)
                return st["w"].pop(i)
            return get

        def wsrc(w, k0c, kcn, c0, n):
            return w[k0c * 128:(k0c + kcn) * 128, c0:c0 + n].rearrange("(k p) n -> p k n", p=128)

        def mm_acc(bk, rows, wv, act_view, kp, kchunks, c0, c1, k_first=0, k_total=None, reads=()):
            k_total = kchunks if k_total is None else k_total

            def f(e):
                for k in range(kchunks):
                    ins = e.matmul(bk.t[0:rows, 0:c1 - c0], wv[0:kp, k, 0:rows], act_view[0:kp, k_first + k, c0:c1],
                                   start=(k_first + k == 0), stop=(k_first + k == k_total - 1))
                return ins
            S.op("pe", f, reads=list(reads), writes=[bk])

        hbq = [TB(None, "hbq%d" % i) for i in range(4)]
        hb.aliases = hbq

        def sumsq_rstd(src, tile):
            nco = ncols_of(tile)
            for q in range(4):
                S.op("act", lambda e, q=q: e.activation(out=hb.t[:, 4 * q:4 * q + 4, 0:nco], in_=src.t[:, 4 * q:4 * q + 4, 0:nco], func=AF.Square),
                     reads=[src], writes=[hbq[q]])
            for (c0, c1) in segs_of(tile):
                bk = S.bank()
                for q in range(4):
                    def f(e, bk=bk, c0=c0, c1=c1, q=q):
                        for c in range(4 * q, 4 * q + 4):
                            ins = e.matmul(bk.t[:, 0:c1 - c0], onesb.t[:], hb.t[:, c, c0:c1], start=(c == 0), stop=(c == KC - 1))
                        return ins
                    S.op("pe", f, reads=[hbq[q], onesb], writes=[bk])
                S.op("act", lambda e, bk=bk, c0=c0, c1=c1: e.activation(out=rstd.t[:, c0:c1], in_=bk.t[:, 0:c1 - c0], func=AF.Sqrt,
                                                                       bias=epsb.t[:, 0:1], scale=1.0 / D),
                     reads=[bk, epsb], writes=[rstd])
            S.op("dve", lambda e: e.reciprocal(out=rstd.t[:, 0:nco], in_=rstd.t[:, 0:nco]), reads=[rstd], writes=[rstd])

        def norm_to(dst, src, gvec, tile):
            nco = ncols_of(tile)
            sumsq_rstd(src, tile)

            def f(e):
                for c in range(KC):
                    ins = e.scalar_tensor_tensor(out=dst.t[:, c, 0:nco], in0=src.t[:, c, 0:nco], scalar=V(gvec, c),
                                                 in1=rstd.t[:, 0:nco], op0=ALU.mult, op1=ALU.mult)
                return ins
            S.op("dve", f, reads=[src, rstd, vecs], writes=[dst])

        def resid_add(src, gvec, tile):
            nco = ncols_of(tile)
            sumsq_rstd(src, tile)

            def f(e):
                for c in range(KC):
                    e.scalar_tensor_tensor(out=src.t[:, c, 0:nco], in0=src.t[:, c, 0:nco], scalar=V(gvec, c),
                                           in1=rstd.t[:, 0:nco], op0=ALU.mult, op1=ALU.mult)
                return e.tensor_tensor(out=x.t[:, :, 0:nco], in0=x.t[:, :, 0:nco], in1=src.t[:, :, 0:nco], op=ALU.add)
            S.op("dve", f, reads=[src, rstd, vecs, x], writes=[src, x])

        def lin(wmat, kchunks, ncols_out, act_view, act_tb, consume, tile):
            segs_ = segs_of(tile)
            wget = wstream([(wsrc(wmat, k0, min(16, kchunks - k0), g0, 128), 128, min(16, kchunks - k0), 128)
                            for g0 in range(0, ncols_out, 128) for k0 in range(0, kchunks, 16)])
            for g0 in range(0, ncols_out, 128):
                banks = [S.bank() for _ in segs_]
                for k0 in range(0, kchunks, 16):
                    kn = min(16, kchunks - k0)
                    wv, wb = wget()
                    for si, (c0, c1) in enumerate(segs_):
                        mm_acc(banks[si], 128, wv, act_view, 128, kn, c0, c1, k_first=k0, k_total=kchunks, reads=[wb, act_tb])
                for si, (c0, c1) in enumerate(segs_):
                    consume(g0 // 128, banks[si], c0, c1)

        def evac_to(dst):
            def consume(ci, bk, c0, c1):
                S.op("act", lambda e: e.activation(out=dst.t[:, ci, c0:c1], in_=bk.t[:, 0:c1 - c0], func=AF.Copy),
                     reads=[bk], writes=[dst])
            return consume

        def load_T(dst3, src_rows, nrows, col0, wr):
            S.dma(xin.t[0:nrows, :], src_rows, writes=[xin], sembuf=xin)
            for g in range(4):
                bk = S.bank()

                def f(e, bk=bk, g=g):
                    for j in range(4):
                        c = 4 * g + j
                        ins = e.transpose(bk.t[:, j * 128:j * 128 + nrows], xin.t[0:nrows, c * 128:(c + 1) * 128],
                                          ident[0:nrows, 0:nrows])
                    return ins
                S.op("pe", f, reads=[xin, cst], writes=[bk])
                S.op("act", lambda e, bk=bk, g=g: e.activation(
                    out=dst3[:, 4 * g:4 * g + 4, col0:col0 + nrows],
                    in_=bk.t[:, :].rearrange("p (j n) -> p j n", n=128)[:, :, 0:nrows], func=AF.Copy),
                    reads=[bk], writes=[wr])

        def store_T(dst_rows, src3, nrows, rd):
            for g in range(4):
                bk = S.bank()

                def f(e, bk=bk, g=g):
                    for j in range(4):
                        c = 4 * g + j
                        ins = e.transpose(bk.t[0:nrows, j * 128:(j + 1) * 128], src3(c), ident)
                    return ins
                S.op("pe", f, reads=[rd, cst], writes=[bk])
                S.op("act", lambda e, bk=bk, g=g: e.activation(out=xin.t[0:nrows, g * 512:(g + 1) * 512],
                                                               in_=bk.t[0:nrows, :], func=AF.Copy),
                     reads=[bk], writes=[xin])
            S.dma(dst_rows, xin.t[0:nrows, :], reads=[xin], sembuf=xin)

        load_T(cacheT.t, cconv, 2 * NS, 0, cacheT)
        load_T(shiftT.t, sshift, NS, 0, shiftT)

        flat = bufA.t[:].rearrange("p c n -> p (c n)")
        tq = [flat[:, i * NCOL:(i + 1) * NCOL] for i in range(14)]
        (R_, K_, V_, LW_, A_, G_, KK_, BON_, CW_, E_, Y_, T1_, T2_, T3_) = tq
        slot_tbs = [TB(None, "slot_" + n) for n in ("R", "K", "V", "LW", "A", "G", "KK", "BON", "CW", "E", "Y", "T1", "T2", "T3")]
        (tR, tK, tV0, tLW, tA, tG0, tKK, tBON, tCW, tE, tY, tT1, tT2, tT3) = slot_tbs
        bufA.aliases = slot_tbs
        Vv, tVv = [V_, tmpc.t[:, 0:NCOL]], [tV0, tmpc]
        Gv, tGv = [G_, tmpd.t[:, 0:NCOL]], [tG0, tmpd]
        xflat = xin.t[:, :]
        SIN = xflat[:, 0:512].rearrange("p (n k) -> p n k", k=64)
        TMS = xflat[:, 512:1024].rearrange("p (n k) -> p n k", k=64)
        DX = xflat[:, 1024:1536].rearrange("p (n k) -> p n k", k=64)
        SA = xflat[:, 1536:1544]

        def do_tile(tile):
            nco = ncols_of(tile)
            segs = segs_of(tile)
            t0 = tile * TT
            for blk in range(4):
                load_T(x.t, xp[t0 + blk * 128:t0 + (blk + 1) * 128, :], 128, blk * 128, x)
            if tile == 0:
                load_T(x.t, xs, NS, TT, x)

            cget = wstream([(wsrc(w_in, 0, 16, off + j * 128, 128), 128, 16, 128) for j in range(KC) for off in (D, 2 * D, 0)])
            norm_to(hb, x, V_MIXPRE + 0, tile)
            zbv = big.t[:, 0:KC * NCOL].rearrange("p (c n) -> p c n", n=NCOL)
            def conv_chunk(j):
                hbk = {}
                for which, off in (("c", D), ("h", 2 * D), ("b", 0)):
                    wv, wb = cget()
                    for si, (c0, c1) in enumerate(segs):
                        bk = S.bank()
                        mm_acc(bk, 128, wv, hb.t, 128, KC, c0, c1, reads=[wb, hb])
                        hbk[(which, si)] = bk
                        if which == "c":
                            S.op("act", lambda e, bk=bk, c0=c0, c1=c1: e.activation(
                                out=bufA.t[:, j, c0:c1], in_=bk.t[:, 0:c1 - c0], func=AF.Copy), reads=[bk], writes=[bufA])
                        elif which == "h":
                            S.op("dve", lambda e, bk=bk, c0=c0, c1=c1: e.tensor_tensor(
                                out=bufA.t[:, j, c0:c1], in0=bufA.t[:, j, c0:c1], in1=bk.t[:, 0:c1 - c0], op=ALU.mult),
                                reads=[bk, bufA], writes=[bufA])
                        elif si == 0:
                            def g(e, bk=bk):
                                e.tensor_copy(out=tmpc.t[:, 0:2], in_=ucarry.t[:, j, :])
                                e.tensor_copy(out=tmpc.t[:, 2:2 + TT], in_=bufA.t[:, j, 0:TT])
                                e.tensor_copy(out=ucarry.t[:, j, :], in_=bufA.t[:, j, TT - 2:TT])
                                e.tensor_scalar(out=tmpd.t[:, 0:TT], in0=tmpc.t[:, 0:TT], scalar1=V(V_CONVW + 0, j), scalar2=0.0, op0=ALU.mult, op1=ALU.add)
                                e.scalar_tensor_tensor(out=tmpd.t[:, 0:TT], in0=tmpc.t[:, 1:1 + TT], scalar=V(V_CONVW + 1, j),
                                                       in1=tmpd.t[:, 0:TT], op0=ALU.mult, op1=ALU.add)
                                e.scalar_tensor_tensor(out=tmpd.t[:, 0:TT], in0=tmpc.t[:, 2:2 + TT], scalar=V(V_CONVW + 2, j),
                                                       in1=tmpd.t[:, 0:TT], op0=ALU.mult, op1=ALU.add)
                                return e.tensor_tensor(out=zbv[:, j, 0:TT], in0=tmpd.t[:, 0:TT], in1=bk.t[:, 0:TT], op=ALU.mult)
                            S.op("dve", g, reads=[bk, bufA, ucarry, vecs, tmpc, tmpd], writes=[tmpc, tmpd, ucarry, big])
                        else:
                            cv = cacheT.t[:, j, :].rearrange("p (n s) -> p s n", s=2)
                            nv = newcs.t[:, j, :].rearrange("p (n s) -> p s n", s=2)
                            S.op("dve", lambda e: e.tensor_scalar(out=tmpd.t[:, 0:NS], in0=cv[:, 0, :], scalar1=V(V_CONVW + 0, j), scalar2=0.0,
                                                                  op0=ALU.mult, op1=ALU.add), reads=[cacheT, vecs], writes=[tmpd])
                            S.op("dve", lambda e: e.scalar_tensor_tensor(out=tmpd.t[:, 0:NS], in0=cv[:, 1, :], scalar=V(V_CONVW + 1, j),
                                                                         in1=tmpd.t[:, 0:NS], op0=ALU.mult, op1=ALU.add),
                                 reads=[cacheT, vecs, tmpd], writes=[tmpd])
                            S.op("dve", lambda e: e.scalar_tensor_tensor(out=tmpd.t[:, 0:NS], in0=bufA.t[:, j, TT:NCOL], scalar=V(V_CONVW + 2, j),
                                                                         in1=tmpd.t[:, 0:NS], op0=ALU.mult, op1=ALU.add),
                                 reads=[bufA, vecs, tmpd], writes=[tmpd])
                            S.op("dve", lambda e: e.tensor_copy(out=nv[:, 0, :], in_=cv[:, 1, :]), reads=[cacheT], writes=[newcs])
                            S.op("dve", lambda e: e.tensor_copy(out=nv[:, 1, :], in_=bufA.t[:, j, TT:NCOL]), reads=[bufA], writes=[newcs])
                            S.op("dve", lambda e, bk=bk: e.tensor_tensor(out=zbv[:, j, TT:NCOL], in0=tmpd.t[:, 0:NS], in1=bk.t[:, 0:NS], op=ALU.mult),
                                 reads=[bk, tmpd], writes=[big])
            for j_ in range(KC):
                conv_chunk(j_)
            lin(w_out, KC, D, zbv, big, evac_to(bufA), tile)
            resid_add(bufA, V_MIXPOST + 0, tile)
            if tile == n_tiles - 1:
                store_T(ncp, lambda c: ucarry.t[:, c, :], 2, ucarry)
            if tile == 0:
                store_T(ncs, lambda c: newcs.t[:, c, :], 2 * NS, newcs)

            def ffn(layer):
                fget = wstream([(wsrc(wm, 0, 16, j * 128, 128), 128, 16, 128) for j in range(FC) for wm in (wg[layer], wu[layer])])
                norm_to(hb, x, V_FFNPRE + layer, tile)
                hT = big.t[:, 0:FC * NCOL].rearrange("p (c n) -> p c n", n=NCOL)
                def fchunk(j):
                    for which, wm in (("g", wg[layer]), ("u", wu[layer])):
                        wv, wb = fget()
                        for si, (c0, c1) in enumerate(segs):
                            bk = S.bank()
                            mm_acc(bk, 128, wv, hb.t, 128, KC, c0, c1, reads=[wb, hb])
                            if which == "g":
                                S.op("act", lambda e, bk=bk, c0=c0, c1=c1: e.activation(
                                    out=tmpc.t[:, c0:c1], in_=bk.t[:, 0:c1 - c0], func=AF.Silu), reads=[bk], writes=[tmpc])
                            else:
                                S.op("dve", lambda e, bk=bk, c0=c0, c1=c1: e.tensor_tensor(
                                    out=hT[:, j, c0:c1], in0=tmpc.t[:, c0:c1], in1=bk.t[:, 0:c1 - c0], op=ALU.mult),
                                    reads=[bk, tmpc], writes=[big])
                for j_ in range(FC):
                    fchunk(j_)
                lin(wd[layer], FC, D, hT, big, evac_to(bufA), tile)
                resid_add(bufA, V_FFNPOST + layer, tile)

            ffn(0)

            norm_to(bufA, x, V_MIXPRE + 1, tile)
            if tile == n_tiles - 1:
                store_T(nsp, lambda c: bufA.t[:, c, TT - 1:TT], 1, bufA)
            if tile == 0:
                store_T(nss, lambda c: bufA.t[:, c, TT:NCOL], NS, bufA)
            mixv = [big.t[:, i * KC * NCOL:(i + 1) * KC * NCOL].rearrange("p (c n) -> p c n", n=NCOL) for i in range(3)]

            def make_mix(dst_view, dst_tb, mi):
                def f(e):
                    for c in range(KC):
                        e.tensor_scalar(out=tmpc.t[:, 0:nco], in0=bufA.t[:, c, 0:nco], scalar1=V(V_OMM + mi, c), scalar2=0.0, op0=ALU.mult, op1=ALU.add)
                        e.scalar_tensor_tensor(out=dst_view[:, c, 1:TT], in0=bufA.t[:, c, 0:TT - 1], scalar=V(V_MIX + mi, c),
                                               in1=tmpc.t[:, 1:TT], op0=ALU.mult, op1=ALU.add)
                        ins = e.scalar_tensor_tensor(out=dst_view[:, c, 0:1], in0=shcarry.t[:, c, :], scalar=V(V_MIX + mi, c),
                                                     in1=tmpc.t[:, 0:1], op0=ALU.mult, op1=ALU.add)
                        if tile == 0:
                            ins = e.scalar_tensor_tensor(out=dst_view[:, c, TT:NCOL], in0=shiftT.t[:, c, :], scalar=V(V_MIX + mi, c),
                                                         in1=tmpc.t[:, TT:NCOL], op0=ALU.mult, op1=ALU.add)
                    return ins
                S.op("dve", f, reads=[bufA, shcarry, shiftT, vecs, tmpc], writes=[tmpc, dst_tb])

            def lora_stage1(mi, wmat, nout, slot, func):
                make_mix(hb.t, hb, mi)
                for q0 in range(0, nout, 128):
                    qn = min(128, nout - q0)
                    wv, wb = load_w(wmat[:, q0:q0 + qn].rearrange("(k p) n -> p k n", p=128), 128, 16, qn)
                    for si, (c0, c1) in enumerate(segs):
                        bk = S.bank()
                        mm_acc(bk, qn, wv, hb.t, 128, KC, c0, c1, reads=[wb, hb])
                        S.op("act", lambda e, bk=bk, q0=q0, qn=qn, c0=c0, c1=c1: e.activation(
                            out=lora.t[0:qn, slot + q0 // 128, c0:c1], in_=bk.t[0:qn, 0:c1 - c0], func=func),
                            reads=[bk], writes=[lora])
            lora_stage1(1, w1, 96, 0, AF.Tanh)
            lora_stage1(4, a1, 96, 1, AF.Copy)
            lora_stage1(5, g1, 256, 2, AF.Sigmoid)
            make_mix(mixv[0], big, 0)
            make_mix(mixv[1], big, 2)
            make_mix(mixv[2], big, 3)
            S.op("dve", lambda e: e.tensor_copy(out=shcarry.t[:, :, :], in_=bufA.t[:, :, TT - 1:TT]), reads=[bufA], writes=[shcarry])

            def P_steps(hp):
                par = hp % 2
                cs = slice(hp * 128, (hp + 1) * 128)
                specs = [
                    (wsrc(w_r, 0, 16, hp * 128, 128), 128, 16, mixv[0], big, R_, tR, AF.Copy, None),
                    (wsrc(w_k, 0, 16, hp * 128, 128), 128, 16, mixv[1], big, K_, tK, AF.Copy, None),
                    (wsrc(w_v, 0, 16, hp * 128, 128), 128, 16, mixv[2], big, Vv[par], tVv[par], AF.Copy, None),
                    (w2[:, cs].rearrange("(k p) n -> p k n", p=96), 96, 1, lora.t[:, 0:1, :], lora, LW_, tLW, AF.Sigmoid, V(V_W0, hp)),
                    (a2[:, cs].rearrange("(k p) n -> p k n", p=96), 96, 1, lora.t[:, 1:2, :], lora, A_, tA, AF.Sigmoid, V(V_A0, hp)),
                    (g2[:, cs].rearrange("(k p) n -> p k n", p=128), 128, 2, lora.t[:, 2:4, :], lora, Gv[par], tGv[par], AF.Copy, None),
                ]

                def mm_ev(spec, wv, wb):
                    (src, kp, kchunks, act_view, act_tb, dstv, dsttb, func, bias) = spec
                    for si, (c0, c1) in enumerate(segs):
                        bk = S.bank()
                        mm_acc(bk, 128, wv, act_view, kp, kchunks, c0, c1, reads=[wb, act_tb])
                        if bias is None:
                            S.op("act", lambda e, bk=bk, c0=c0, c1=c1: e.activation(out=dstv[:, c0:c1], in_=bk.t[:, 0:c1 - c0], func=func),
                                 reads=[bk], writes=[dsttb])
                        else:
                            S.op("act", lambda e, bk=bk, c0=c0, c1=c1: e.activation(out=dstv[:, c0:c1], in_=bk.t[:, 0:c1 - c0], func=func,
                                                                                   bias=bias),
                                 reads=[bk, vecs], writes=[dsttb])
                n_sp = len(specs)
                h = load_w_dma(specs[0][0], specs[0][1], specs[0][2], 128)
                yield
                wprev = None
                for i in range(n_sp):
                    wcur = load_w_cast(h)
                    if i + 1 < n_sp:
                        nx = specs[i + 1]
                        h = load_w_dma(nx[0], nx[1], nx[2], 128)
                    if wprev is not None:
                        mm_ev(specs[i - 1], wprev[0], wprev[1])
                    wprev = wcur
                    yield
                mm_ev(specs[n_sp - 1], wprev[0], wprev[1])
                yield


            def bd_sum(src_view, src_tb, cb):
                for si, (c0, c1) in enumerate(segs):
                    bk = S.bank()
                    S.op("pe", lambda e, bk=bk, c0=c0, c1=c1: e.matmul(bk.t[:, 0:c1 - c0], BDm, src_view[:, c0:c1], start=True, stop=True),
                         reads=[src_tb, cst], writes=[bk])
                    cb(bk, c0, c1)

            def dve(fn, reads=(), writes=()):
                S.op("dve", fn, reads=list(reads), writes=list(writes))

            def act(fn, reads=(), writes=()):
                S.op("act", fn, reads=list(reads), writes=list(writes))

            def Q_gen(hp):
                par = hp % 2
                V_, tV, G_, tG = Vv[par], tVv[par], Gv[par], tGv[par]
                n_ = nco
                dve(lambda e: e.tensor_scalar(out=LW_[:, 0:n_], in0=LW_[:, 0:n_], scalar1=-0.6065306597126334, scalar2=0.0, op0=ALU.mult, op1=ALU.add),
                    reads=[tLW], writes=[tLW])
                yield
                dve(lambda e: e.tensor_scalar(out=KK_[:, 0:n_], in0=K_[:, 0:n_], scalar1=V(V_KK, hp), scalar2=0.0, op0=ALU.mult, op1=ALU.add),
                    reads=[tK, vecs], writes=[tKK])
                yield
                dve(lambda e: e.tensor_tensor(out=T1_[:, 0:n_], in0=KK_[:, 0:n_], in1=KK_[:, 0:n_], op=ALU.mult), reads=[tKK], writes=[tT1])
                yield

                def kkn(bk, c0, c1):
                    S.op("dve", lambda e: e.tensor_scalar(out=T2_[:, c0:c1], in0=bk.t[:, 0:c1 - c0], scalar1=1e-24, scalar2=0.0, op0=ALU.max, op1=ALU.add),
                         reads=[bk], writes=[tT2])
                bd_sum(T1_, tT1, kkn)
                yield
                act(lambda e: e.activation(out=T2_[:, 0:n_], in_=T2_[:, 0:n_], func=AF.Sqrt), reads=[tT2], writes=[tT2])
                yield
                dve(lambda e: e.reciprocal(out=T2_[:, 0:n_], in_=T2_[:, 0:n_]), reads=[tT2], writes=[tT2])
                yield
                dve(lambda e: e.tensor_tensor(out=KK_[:, 0:n_], in0=KK_[:, 0:n_], in1=T2_[:, 0:n_], op=ALU.mult), reads=[tKK, tT2], writes=[tKK])
                yield
                dve(lambda e: e.tensor_scalar(out=T1_[:, 0:n_], in0=A_[:, 0:n_], scalar1=-1.0, scalar2=V(V_KA, hp), op0=ALU.add, op1=ALU.mult),
                    reads=[tA, vecs], writes=[tT1])
                yield
                dve(lambda e: e.scalar_tensor_tensor(out=K_[:, 0:n_], in0=T1_[:, 0:n_], scalar=1.0, in1=K_[:, 0:n_], op0=ALU.add, op1=ALU.mult),
                    reads=[tT1, tK], writes=[tK])
                yield
                dve(lambda e: e.tensor_tensor(out=A_[:, 0:n_], in0=KK_[:, 0:n_], in1=A_[:, 0:n_], op=ALU.mult), reads=[tKK, tA], writes=[tA])
                yield
                dve(lambda e: e.scalar_tensor_tensor(out=T1_[:, 0:n_], in0=R_[:, 0:n_], scalar=V(V_RK, hp), in1=K_[:, 0:n_],
                                                     op0=ALU.mult, op1=ALU.mult), reads=[tR, tK, vecs], writes=[tT1])
                yield

                def bon(bk, c0, c1):
                    S.op("dve", lambda e: e.tensor_tensor(out=BON_[:, c0:c1], in0=V_[:, c0:c1], in1=bk.t[:, 0:c1 - c0], op=ALU.mult),
                         reads=[bk, tV], writes=[tBON])
                bd_sum(T1_, tT1, bon)
                yield

                for ch in range(TT // CH):
                    a0, a1_ = ch * CH, (ch + 1) * CH
                    dve(lambda e, a0=a0, a1_=a1_: e.tensor_tensor_scan(out=CW_[:, a0:a1_], data0=onesf.t[:, 0:CH], data1=LW_[:, a0:a1_],
                                                                      initial=0.0, op0=ALU.mult, op1=ALU.add), reads=[onesf, tLW], writes=[tCW])
                act(lambda e: e.activation(out=E_[:, 0:TT], in_=CW_[:, 0:TT], func=AF.Exp), reads=[tCW], writes=[tE])
                yield
                dve(lambda e: e.tensor_tensor(out=T3_[:, 0:TT], in0=R_[:, 0:TT], in1=E_[:, 0:TT], op=ALU.mult), reads=[tR, tE], writes=[tT3])
                yield
                dve(lambda e: e.tensor_copy(out=wcT.t[:, :], in_=E_[:, 0:TT].rearrange("p (c t) -> p c t", t=CH)[:, :, CH - 1]),
                    reads=[tE], writes=[wcT])
                yield
                act(lambda e: e.activation(out=E_[:, 0:TT], in_=CW_[:, 0:TT], func=AF.Exp, scale=-1.0), reads=[tCW], writes=[tE])
                yield
                dve(lambda e: e.tensor_tensor(out=T1_[:, 0:TT], in0=A_[:, 0:TT], in1=E_[:, 0:TT], op=ALU.mult), reads=[tA, tE], writes=[tT1])
                yield
                dve(lambda e: e.tensor_tensor(out=T2_[:, 0:TT], in0=K_[:, 0:TT], in1=E_[:, 0:TT], op=ALU.mult), reads=[tK, tE], writes=[tT2])
                yield
                dve(lambda e: e.tensor_tensor(out=E_[:, 0:TT], in0=CW_[:, 0:TT], in1=LW_[:, 0:TT], op=ALU.subtract), reads=[tCW, tLW], writes=[tE])
                yield
                act(lambda e: e.activation(out=E_[:, 0:TT], in_=E_[:, 0:TT], func=AF.Exp), reads=[tE], writes=[tE])
                yield
                dve(lambda e: e.scalar_tensor_tensor(out=E_[:, 0:TT], in0=KK_[:, 0:TT], scalar=-1.0, in1=E_[:, 0:TT],
                                                     op0=ALU.mult, op1=ALU.mult), reads=[tKK, tE], writes=[tE])
                yield
                AT_, BT_, KT_, RT_ = E_, T1_, T2_, T3_

                yield

            def mid(hp):
                par = hp % 2
                V_, tV, G_, tG = Vv[par], tVv[par], Gv[par], tGv[par]
                n_ = nco
                AT_, BT_, KT_, RT_ = E_, T1_, T2_, T3_
                if tile == 0:
                    def sample_half(half):
                        n0 = half * 8
                        sc0_, sc1_ = TT + n0, TT + n0 + 8
                        src = swkv[n0:n0 + 8, 2 * hp:2 * hp + 2, :, :].rearrange("n h v k -> (h v) n k")
                        dst = nws[n0:n0 + 8, 2 * hp:2 * hp + 2, :, :].rearrange("n h v k -> (h v) n k")
                        S.dma(SIN, src, writes=[xin], sembuf=xin)

                        def bcast(srcv, srctb, cb):
                            S.op("dve", lambda e: e.tensor_tensor(out=DX, in0=srcv.unsqueeze(2).broadcast_to([128, 8, 64]),
                                                                  in1=ident2.unsqueeze(1).broadcast_to([128, 8, 64]), op=ALU.mult),
                                 reads=[srctb, cst, xin], writes=[xin])
                            bk = S.bank()
                            S.op("pe", lambda e, bk=bk: e.matmul(bk.t[:, 0:512], BDm, DX.rearrange("p n k -> p (n k)"), start=True, stop=True),
                                 reads=[xin, cst], writes=[bk])
                            cb(bk.t[:, 0:512].rearrange("p (n k) -> p n k", k=64), bk)

                        def c_a(pv, bk):
                            def f(e):
                                e.tensor_tensor(out=TMS, in0=SIN, in1=pv, op=ALU.mult)
                                return e.tensor_reduce(out=SA, in_=TMS, axis=AX.X, op=ALU.add, negate=True)
                            S.op("dve", f, reads=[bk, xin], writes=[xin])
                        bcast(KK_[:, sc0_:sc1_], tKK, c_a)

                        def c_w(pv, bk):
                            S.op("act", lambda e: e.activation(out=TMS, in_=pv, func=AF.Exp), reads=[bk, xin], writes=[xin])
                            S.op("dve", lambda e: e.tensor_tensor(out=SIN, in0=SIN, in1=TMS, op=ALU.mult), reads=[xin], writes=[xin])
                        bcast(LW_[:, sc0_:sc1_], tLW, c_w)

                        def c_b(pv, bk):
                            def f(e):
                                e.tensor_tensor(out=TMS, in0=pv, in1=SA.unsqueeze(2).broadcast_to([128, 8, 64]), op=ALU.mult)
                                return e.tensor_tensor(out=SIN, in0=SIN, in1=TMS, op=ALU.add)
                            S.op("dve", f, reads=[bk, xin], writes=[xin])
                        bcast(A_[:, sc0_:sc1_], tA, c_b)

                        def c_k(pv, bk):
                            def f(e):
                                e.tensor_tensor(out=TMS, in0=pv, in1=V_[:, sc0_:sc1_].unsqueeze(2).broadcast_to([128, 8, 64]), op=ALU.mult)
                                return e.tensor_tensor(out=SIN, in0=SIN, in1=TMS, op=ALU.add)
                            S.op("dve", f, reads=[bk, xin, tV], writes=[xin])
                        bcast(K_[:, sc0_:sc1_], tK, c_k)

                        def c_r(pv, bk):
                            def f(e):
                                e.tensor_tensor(out=TMS, in0=SIN, in1=pv, op=ALU.mult)
                                return e.tensor_reduce(out=Y_[:, sc0_:sc1_], in_=TMS, axis=AX.X, op=ALU.add)
                            S.op("dve", f, reads=[bk, xin], writes=[xin, tY])
                        bcast(R_[:, sc0_:sc1_], tR, c_r)
                        S.dma(dst, SIN, reads=[xin], sembuf=xin)

                    for half_ in range(2):
                        sample_half(half_)

                def fcast(e):
                    e.tensor_copy(out=AB3.t[:, 0:TT], in_=AT_[:, 0:TT])
                    e.tensor_copy(out=AB3.t[:, TT:2 * TT], in_=BT_[:, 0:TT])
                    return e.tensor_copy(out=AB3.t[:, 2 * TT:3 * TT], in_=RT_[:, 0:TT])
                S.op("pool", fcast, reads=[tE, tT1, tT3], writes=[AB3])
                S.op("act", lambda e: e.activation(out=S0Tb.t[:, :], in_=S0T.t[:, hp, :], func=AF.Copy), reads=[S0T], writes=[S0Tb])

                def stageA(ch):
                    p = ch % 3
                    a0, a1_ = ch * CH, (ch + 1) * CH
                    ex, tk, ms, xw, xwb, tat = EX[p], TK[p], MS[p], XW[p], XWB[p], TAT[p]
                    ATx, BTx, KTx, RTx, Vx = (ex.t[:, i * 128:(i + 1) * 128] for i in range(5))
                    N_bd, AakT_bd, A_bd, ArbT_bd, ArkT_bd = (ms.t[:, i * 128:(i + 1) * 128] for i in range(5))

                    def bc2(v, off=0):
                        return v[:, off + a0:off + a1_].unsqueeze(1).broadcast_to([128, 2, CH])
                    bdv = BDm.rearrange("p (h t) -> p h t", h=2)

                    def fex(e):
                        for i, src in enumerate((AT_, BT_, KT_, RT_, V_)):
                            ins = e.tensor_tensor(out=ex.t[:, i * 128:(i + 1) * 128].rearrange("p (h t) -> p h t", h=2), in0=bc2(src), in1=bdv,
                                                  op=ALU.mult)
                        return ins
                    S.op("pool", fex, reads=[tE, tT1, tT2, tT3, tV, cst], writes=[ex])
                    bkt = S.bank()

                    def ftr(e):
                        e.matmul(bkt.t[:, 0:128], BTx, identb.t[:, 0:128], start=True, stop=True)
                        e.matmul(bkt.t[:, 128:256], KTx, identb.t[:, 0:128], start=True, stop=True)
                        return e.matmul(bkt.t[:, 256:320], Vx, identb.t[:, 128:192], start=True, stop=True)
                    S.op("pe", ftr, reads=[ex, identb], writes=[bkt])
                    S.op("act", lambda e: e.activation(out=tk.t[:, :], in_=bkt.t[:, 0:320], func=AF.Copy), reads=[bkt], writes=[tk])
                    bka = S.bank()
                    bkb = S.bank()
                    at2, bt2, rt2 = bc2(AB3.t, 0), bc2(AB3.t, TT), bc2(AB3.t, 2 * TT)

                    def fA(e):
                        e.matmul(bka.t[:, 0:128], BTx, at2, start=True, stop=True)
                        e.matmul(bka.t[:, 128:256], KTx, at2, start=True, stop=True)
                        e.matmul(bka.t[:, 256:384], ATx, bt2, start=True, stop=True)
                        e.matmul(bkb.t[:, 0:128], BTx, rt2, start=True, stop=True)
                        return e.matmul(bkb.t[:, 128:256], KTx, rt2, start=True, stop=True)
                    S.op("pe", fA, reads=[ex, AB3], writes=[bka, bkb])

                    def fm(e):
                        e.tensor_tensor(out=ms.t[:, 0:256].rearrange("p (a n) -> p a n", a=2), in0=bka.t[:, 0:256].rearrange("p (a n) -> p a n", a=2),
                                        in1=msu_bd.unsqueeze(1).broadcast_to([128, 2, 128]), op=ALU.mult)
                        e.tensor_tensor(out=A_bd, in0=bka.t[:, 256:384], in1=msl_bd, op=ALU.mult)
                        return e.tensor_tensor(out=ms.t[:, 384:640].rearrange("p (a n) -> p a n", a=2),
                                               in0=bkb.t[:, 0:256].rearrange("p (a n) -> p a n", a=2),
                                               in1=msi_bd.unsqueeze(1).broadcast_to([128, 2, 128]), op=ALU.mult)
                    S.op("dve", fm, reads=[bka, bkb, cst], writes=[ms])
                    yield
                    bkr = S.bank()

                    def fr(e):
                        e.matmul(bkr.t[:, 0:64], AakT_bd, tk.t[:, 256:320], start=True, stop=True)
                        return e.matmul(bkr.t[:, 64:192], ATx, identb.t[:, 0:128], start=True, stop=True)
                    S.op("pe", fr, reads=[ms, tk, ex, identb], writes=[bkr])
                    S.op("dve", lambda e: e.tensor_copy(out=xwb.t[:, :], in_=bkr.t[:, 0:192]), reads=[bkr], writes=[xwb])
                    S.op("dve", lambda e: e.tensor_copy(out=xw.t[:, :], in_=bkr.t[:, 0:192]), reads=[bkr], writes=[xw])
                    yield
                    Pc, Mc, cur = A_bd, N_bd, ms
                    for j in range(6):
                        bkx = S.bank()
                        S.op("pe", lambda e, bkx=bkx, Mc=Mc: e.matmul(bkx.t[:, 0:192], Mc, xwb.t[:, :], start=True, stop=True),
                             reads=[cur, xwb], writes=[bkx])
                        if j < 5:
                            bkp = S.bank()
                            nxt = PMA[p] if j % 2 == 0 else PMB[p]

                            def fsq(e, bkp=bkp, Mc=Mc, Pc=Pc):
                                e.matmul(bkp.t[:, 0:128], Mc, Pc, start=True, stop=True)
                                return e.matmul(bkp.t[:, 128:256], Pc, Mc, start=True, stop=True)
                            S.op("pe", fsq, reads=[cur], writes=[bkp])
                        S.op("dve", lambda e, bkx=bkx: e.tensor_tensor(out=xwb.t[:, :], in0=xw.t[:, :], in1=bkx.t[:, 0:192], op=ALU.add),
                             reads=[bkx, xw], writes=[xwb])
                        S.op("dve", lambda e, bkx=bkx: e.tensor_tensor(out=xw.t[:, :], in0=xw.t[:, :], in1=bkx.t[:, 0:192], op=ALU.add),
                             reads=[bkx, xw], writes=[xw])
                        if j < 5:
                            S.op("act", lambda e, bkp=bkp, nxt=nxt: e.activation(out=nxt.t[:, :], in_=bkp.t[:, 0:256], func=AF.Copy),
                                 reads=[bkp], writes=[nxt])
                            Pc, Mc, cur = nxt.t[:, 0:128], nxt.t[:, 128:256], nxt
                        yield
                    bkq = S.bank()
                    S.op("pe", lambda e: e.matmul(bkq.t[:, 0:128], xwb.t[:, 64:192], identb.t[:, 0:128], start=True, stop=True),
                         reads=[xwb, identb], writes=[bkq])
                    S.op("act", lambda e: e.activation(out=tat.t[:, :], in_=bkq.t[:, 0:128], func=AF.Copy), reads=[bkq], writes=[tat])
                    yield

                def stageB(ch):
                    p = ch % 3
                    a0, a1_ = ch * CH, (ch + 1) * CH
                    ex, tk, ms, xw, tat, uu, yx = EX[p], TK[p], MS[p], XW[p], TAT[p], UU[ch % 2], YX[ch % 2]
                    RTx = ex.t[:, 384:512]
                    ArbT_bd, ArkT_bd = ms.t[:, 384:512], ms.t[:, 512:640]
                    s0 = S0T.t[:, hp, :]
                    bku = S.bank()
                    S.op("pe", lambda e: e.matmul(bku.t[:, 0:64], tat.t[:, :], S0Tb.t[:, :], start=True, stop=True), reads=[tat, S0Tb], writes=[bku])
                    S.op("dve", lambda e: e.tensor_tensor(out=uu.t[:, :], in0=xw.t[:, 0:64], in1=bku.t[:, 0:64], op=ALU.add),
                         reads=[bku, xw], writes=[uu])
                    S.op("dve", lambda e: e.tensor_scalar(out=s0, in0=s0, scalar1=wcT.t[:, ch:ch + 1], scalar2=0.0, op0=ALU.mult, op1=ALU.add),
                         reads=[S0T, wcT], writes=[S0T])
                    yield
                    bky = S.bank()
                    bks = S.bank()

                    def fy(e):
                        e.matmul(bky.t[:, 0:64], RTx, S0Tb.t[:, :], start=True, stop=False)
                        e.matmul(bky.t[:, 0:64], ArbT_bd, uu.t[:, :], start=False, stop=False)
                        return e.matmul(bky.t[:, 0:64], ArkT_bd, tk.t[:, 256:320], start=False, stop=True)
                    S.op("pe", fy, reads=[ex, S0Tb, ms, uu, tk], writes=[bky])

                    def fs(e):
                        e.matmul(bks.t[:, 0:64], tk.t[:, 0:128], uu.t[:, :], start=True, stop=False)
                        return e.matmul(bks.t[:, 0:64], tk.t[:, 128:256], tk.t[:, 256:320], start=False, stop=True)
                    S.op("pe", fs, reads=[tk, uu], writes=[bks])
                    S.op("dve", lambda e: e.scalar_tensor_tensor(out=S0Tb.t[:, :], in0=bks.t[:, 0:64], scalar=wcT.t[:, ch:ch + 1], in1=s0,
                                                                 op0=ALU.mult, op1=ALU.add), reads=[bks, wcT, S0T], writes=[S0Tb])
                    S.op("dve", lambda e: e.scalar_tensor_tensor(out=s0, in0=bks.t[:, 0:64], scalar=wcT.t[:, ch:ch + 1], in1=s0,
                                                                 op0=ALU.mult, op1=ALU.add), reads=[bks, wcT, S0T], writes=[S0T])
                    S.op("dve", lambda e: e.tensor_tensor(out=yx.t[:, :].rearrange("p (h v) -> p h v", h=2),
                                                          in0=bky.t[:, 0:64].unsqueeze(1).broadcast_to([128, 2, 64]),
                                                          in1=BDm.rearrange("p (h t) -> p h t", h=2), op=ALU.mult),
                         reads=[bky, cst], writes=[yx])
                    yield
                    bkz = S.bank()
                    S.op("pe", lambda e: e.matmul(bkz.t[:, 0:64], yx.t[:, :], ident2, start=True, stop=True), reads=[yx, cst], writes=[bkz])
                    S.op("act", lambda e: e.activation(out=Y_[:, a0:a1_], in_=bkz.t[:, 0:64], func=AF.Copy), reads=[bkz], writes=[tY])
                    yield

                NCH = TT // CH
                active = []
                nextA, nextB, doneA, doneB = 0, 0, set(), 0
                pg = P_steps(hp + 1) if hp + 1 < KC else None
                tick = 0
                while doneB < NCH:
                    tick += 1
                    if pg is not None and tick % 4 == 0:
                        try:
                            next(pg)
                        except StopIteration:
                            pg = None
                    nA = sum(1 for a in active if a[0] == "A")
                    if nextA < NCH and nA < 2 and nextA <= doneB + 2:
                        active.append(["A", nextA, stageA(nextA)])
                        nextA += 1
                    if nextB < NCH and nextB in doneA and nextB == doneB and not any(a[0] == "B" for a in active):
                        active.append(["B", nextB, stageB(nextB)])
                        nextB += 1
                    for a in list(active):
                        try:
                            next(a[2])
                        except StopIteration:
                            active.remove(a)
                            if a[0] == "A":
                                doneA.add(a[1])
                            else:
                                doneB += 1
                if pg is not None:
                    for _ in pg:
                        pass

            def N_gen(hp):
                par = hp % 2
                G_, tG = Gv[par], tGv[par]
                n_ = nco

                def gn1(bk, c0, c1):
                    S.op("dve", lambda e: e.scalar_tensor_tensor(out=Y_[:, c0:c1], in0=bk.t[:, 0:c1 - c0], scalar=-1.0 / 64, in1=Y_[:, c0:c1],
                                                                 op0=ALU.mult, op1=ALU.add), reads=[bk, tY], writes=[tY])
                bd_sum(Y_, tY, gn1)
                yield
                dve(lambda e: e.tensor_tensor(out=CW_[:, 0:n_], in0=Y_[:, 0:n_], in1=Y_[:, 0:n_], op=ALU.mult), reads=[tY], writes=[tCW])
                yield

                def gn2(bk, c0, c1):
                    S.op("act", lambda e: e.activation(out=CW_[:, c0:c1], in_=bk.t[:, 0:c1 - c0], func=AF.Sqrt, bias=epsb.t[:, 1:2], scale=1.0 / 64),
                         reads=[bk, epsb], writes=[tCW])
                bd_sum(CW_, tCW, gn2)
                yield
                dve(lambda e: e.reciprocal(out=CW_[:, 0:n_], in_=CW_[:, 0:n_]), reads=[tCW], writes=[tCW])
                yield
                dve(lambda e: e.tensor_tensor(out=Y_[:, 0:n_], in0=Y_[:, 0:n_], in1=CW_[:, 0:n_], op=ALU.mult), reads=[tY, tCW], writes=[tY])
                yield
                dve(lambda e: e.tensor_scalar(out=Y_[:, 0:n_], in0=Y_[:, 0:n_], scalar1=V(V_LNW, hp), scalar2=V(V_LNB, hp),
                                              op0=ALU.mult, op1=ALU.add), reads=[tY, vecs], writes=[tY])
                yield
                dve(lambda e: e.tensor_tensor(out=Y_[:, 0:n_], in0=Y_[:, 0:n_], in1=BON_[:, 0:n_], op=ALU.add), reads=[tY, tBON], writes=[tY])
                yield
                dve(lambda e: e.tensor_tensor(out=hb.t[:, hp, 0:n_], in0=Y_[:, 0:n_], in1=G_[:, 0:n_], op=ALU.mult), reads=[tY, tG], writes=[hb])
                yield

            def rr(gens):
                alive = list(gens)
                while alive:
                    for g in list(alive):
                        try:
                            next(g)
                        except StopIteration:
                            alive.remove(g)

            for _ in P_steps(0):
                pass
            rr([Q_gen(0)])
            for hp_ in range(KC):
                mid(hp_)
                rr([N_gen(hp_)] + ([Q_gen(hp_ + 1)] if hp_ + 1 < KC else []))
            lin(w_o, KC, D, hb.t, hb, evac_to(bufA), tile)
            resid_add(bufA, V_MIXPOST + 1, tile)
            ffn(1)

            for blk in range(4):
                store_T(yp[t0 + blk * 128:t0 + (blk + 1) * 128, :], lambda c, blk=blk: x.t[:, c, blk * 128:(blk + 1) * 128], 128, x)
            if tile == 0:
                store_T(ysd, lambda c: x.t[:, c, TT:NCOL], NS, x)

        for tile_ in range(n_tiles):
            do_tile(tile_)

        for hp in range(KC):
            bk = S.bank()
            S.op("pe", lambda e, bk=bk, hp=hp: e.transpose(bk.t[0:64, 0:128], S0T.t[:, hp, :], ident), reads=[S0T, cst], writes=[bk])
            S.op("act", lambda e, bk=bk, hp=hp: e.activation(out=xin.t[0:64, hp * 128:(hp + 1) * 128], in_=bk.t[0:64, 0:128], func=AF.Copy),
                 reads=[bk], writes=[xin])
        S.dma(nwp.rearrange("(c h) v k -> v c h k", h=2), xin.t[0:64, :].rearrange("v (c h k) -> v c h k", h=2, k=64), reads=[xin], sembuf=xin)

        allt = [x, bufA, hb, big, xin, vecs, cst] + wst + wbf
        final_waits = []
        for tb in allt:
            if tb.dsem is not None:
                final_waits.append((tb.dsem, 16 * tb.ndma))
        with nc.Block() as block:
            S.emit(block, final_waits)
    return nc


def _pack_vec(v):
    return np.ascontiguousarray(np.asarray(v, np.float32).reshape(KC, 128).T)


def _consts():
    c = np.zeros((128, NCST), np.float32)
    c[:, 0:128] = np.eye(128)
    c[0:64, 128:192] = 1.0
    c[64:128, 192:256] = 1.0
    su = np.triu(np.ones((64, 64), np.float32), 1)
    c[0:64, 256:320] = su
    c[0:64, 320:384] = np.triu(np.ones((64, 64), np.float32), 0)
    c[0:64, 384:448] = su.T
    c[0:64, 448 + 64:448 + 128] = np.eye(64)
    c[0:64, 576:640] = np.eye(64)
    c[64:128, 576:640] = np.eye(64)
    si = np.triu(np.ones((64, 64), np.float32), 0)
    for h in range(2):
        sl_ = slice(64 * h, 64 * h + 64)
        c[sl_, 640 + 64 * h:640 + 64 * h + 64] = su
        c[sl_, 768 + 64 * h:768 + 64 * h + 64] = si
        c[sl_, 896 + 64 * h:896 + 64 * h + 64] = su.T
    return c


_NC_CACHE = {}


def kernel(x_prompt, x_sample, cache_conv, state_wkv, state_shift,
           norm_mix_pre, norm_mix_post, norm_ffn_pre, norm_ffn_post,
           conv_w_in, conv_w, conv_w_out,
           rwkv_mix, rwkv_w_r, rwkv_w_k, rwkv_w_v, rwkv_w_o,
           rwkv_w0, rwkv_w1, rwkv_w2, rwkv_a0, rwkv_a1, rwkv_a2,
           rwkv_g1, rwkv_g2, rwkv_k_k, rwkv_k_a, rwkv_r_k, rwkv_ln_w, rwkv_ln_b,
           ffn_w_gate, ffn_w_up, ffn_w_down):
    f = lambda a: np.ascontiguousarray(np.asarray(a, np.float32))
    vl = [None] * NV
    for i in range(2):
        vl[V_MIXPRE + i] = norm_mix_pre[i]
        vl[V_MIXPOST + i] = norm_mix_post[i]
        vl[V_FFNPRE + i] = norm_ffn_pre[i]
        vl[V_FFNPOST + i] = norm_ffn_post[i]
    for i in range(3):
        vl[V_CONVW + i] = conv_w[0, i]
    for i in range(6):
        vl[V_MIX + i] = rwkv_mix[0, i]
    vl[V_W0] = rwkv_w0[0]
    vl[V_A0] = rwkv_a0[0]
    vl[V_KK] = rwkv_k_k[0]
    vl[V_KA] = rwkv_k_a[0]
    vl[V_RK] = np.asarray(rwkv_r_k[0]).reshape(-1)
    vl[V_LNW] = rwkv_ln_w[0]
    vl[V_LNB] = rwkv_ln_b[0]
    vecs = np.zeros((128, NV, KC), np.float32)
    for i, v in enumerate(vl):
        if v is not None:
            vecs[:, i, :] = _pack_vec(v)
    vecs = vecs.reshape(128, NV * KC)
    shared = {
        "vecs": vecs, "cst": _consts(),
        "w_in": f(conv_w_in[0]), "w_out": f(conv_w_out[0]),
        "w_r": f(rwkv_w_r[0]), "w_k": f(rwkv_w_k[0]), "w_v": f(rwkv_w_v[0]), "w_o": f(rwkv_w_o[0]),
        "w1": f(rwkv_w1[0]), "w2": f(rwkv_w2[0]), "a1": f(rwkv_a1[0]), "a2": f(rwkv_a2[0]),
        "g1": f(rwkv_g1[0]), "g2": f(rwkv_g2[0]),
        "wg0": f(ffn_w_gate[0]), "wg1": f(ffn_w_gate[1]),
        "wu0": f(ffn_w_up[0]), "wu1": f(ffn_w_up[1]),
        "wd0": f(ffn_w_down[0]), "wd1": f(ffn_w_down[1]),
    }
    in_maps = []
    for c in range(8):
        b = c % 4
        m = dict(shared)
        m["xp"] = f(x_prompt[b])
        m["xs"] = f(x_sample[c * NS:(c + 1) * NS, 0])
        m["cconv"] = f(cache_conv[0, c * NS:(c + 1) * NS]).reshape(NS * 2, D)
        m["swkv"] = f(state_wkv[0, c * NS:(c + 1) * NS])
        m["sshift"] = f(state_shift[0, c * NS:(c + 1) * NS])
        in_maps.append(m)
    if "nc" not in _NC_CACHE:
        _NC_CACHE["nc"] = build_program()
    res = run_bass_kernel_spmd(_NC_CACHE["nc"], in_maps, core_ids=list(range(8)))
    R = res.results
    y_prompt = np.stack([R[b]["yp"] for b in range(4)]).astype(np.float32)
    y_sample = np.concatenate([R[c]["ys"] for c in range(8)]).reshape(128, 1, D).astype(np.float32)
    conv_p = np.stack([R[b]["ncp"] for b in range(4)])[None].astype(np.float32)
    conv_s = np.concatenate([R[c]["ncs"].reshape(NS, 2, D) for c in range(8)])[None].astype(np.float32)
    wkv_p = np.stack([R[b]["nwp"] for b in range(4)])[None].astype(np.float32)
    wkv_s = np.concatenate([R[c]["nws"] for c in range(8)])[None].astype(np.float32)
    shift_p = np.stack([R[b]["nsp"].reshape(D) for b in range(4)])[None].astype(np.float32)
    shift_s = np.concatenate([R[c]["nss"] for c in range(8)])[None].astype(np.float32)
    return (y_prompt, y_sample, conv_p, conv_s, wkv_p, wkv_s, shift_p, shift_s)
```
